# Optimizing a Trainium2 kernel written in Bass

```python
import math
import jax, jax.numpy as jnp
from jax import lax
import numpy as np

D_MODEL = 4096
BATCH = 4
SEQ = 4096
DEPTH = 1

CHUNK = 64
D_FF = 256 * ((8 * D_MODEL // 3 + 255) // 256)
GMLP_WIDTH = D_MODEL // 2
GMLP_GROUP_DIM = 128
GMLP_GROUPS = GMLP_WIDTH // GMLP_GROUP_DIM
GMLP_BLOCK = 128
HEAD_DIM = 128
ATTN_HEADS = (D_MODEL // 2) // HEAD_DIM
ATTN_WIDTH = ATTN_HEADS * HEAD_DIM
IDX_HEADS = D_MODEL // 128
IDX_DIM = 64
TOPK_MAX = 256
Q_BLOCK = 128
ROPE_THETA = 10000.0
EPS = 1e-6
SPLITS = (GMLP_WIDTH, GMLP_WIDTH, ATTN_WIDTH, HEAD_DIM, HEAD_DIM,
          IDX_HEADS * IDX_DIM, IDX_DIM, IDX_HEADS, D_MODEL, D_MODEL)
IN_WIDTH = sum(SPLITS)

kernel_name = "hybrid_gmlp_dsa_macaron_block"


def rms_norm(x, g):
    xf = x.astype(jnp.float32)
    y = xf * lax.rsqrt(jnp.mean(xf * xf, axis=-1, keepdims=True) + EPS)
    return (y * g.astype(jnp.float32)).astype(x.dtype)


def rope_tables(seq, dim):
    inv = ROPE_THETA ** (-jnp.arange(0, dim, 2, dtype=jnp.float32) / dim)
    ang = jnp.arange(seq, dtype=jnp.float32)[:, None] * inv[None, :]
    return jnp.cos(ang), jnp.sin(ang)


def apply_rope(x, cos, sin):
    if x.ndim == 4:
        cos, sin = cos[:, None, :], sin[:, None, :]
    xf = x.astype(jnp.float32)
    x1, x2 = jnp.split(xf, 2, axis=-1)
    out = jnp.concatenate([x1 * cos - x2 * sin, x1 * sin + x2 * cos], axis=-1)
    return out.astype(x.dtype)


def swiglu(x, w1, w3, w2):
    return (jax.nn.silu(x @ w1) * (x @ w3)) @ w2


def gmlp_spatial_gating(u, v, v_gain, ws, bs):
    B, S, _ = u.shape
    u = jax.nn.gelu(u, approximate=False)
    v = rms_norm(jax.nn.gelu(v, approximate=False), v_gain)
    n_blk = S // GMLP_BLOCK
    pos_chunk = jnp.arange(GMLP_BLOCK) // CHUNK
    mask = pos_chunk[None, :] <= pos_chunk[:, None]
    ws_m = jnp.where(mask[None], ws, jnp.zeros_like(ws))
    vb = v.reshape(B, n_blk, GMLP_BLOCK, GMLP_GROUPS, GMLP_GROUP_DIM)
    mixed = jnp.einsum('gts,bnsgc->bntgc', ws_m, vb) + bs.T[None, None, :, :, None]
    return u * mixed.reshape(B, S, GMLP_WIDTH)


def dsa_attention(q, k, v, q_idx, k_idx, w_idx):
    B, S, H, hd = q.shape
    n_blk = S // Q_BLOCK
    top = min(TOPK_MAX, S // 4)
    key_chunk = jnp.arange(S) // CHUNK
    gather = jax.vmap(lambda tab, ids: tab[ids])

    def one_block(args):
        qb, qib, wb, blk = args
        q_chunk = (blk * Q_BLOCK + jnp.arange(Q_BLOCK)) // CHUNK
        rel = jax.nn.relu(jnp.einsum('bthd,bsd->bths', qib, k_idx).astype(jnp.float32))
        score = jnp.einsum('bth,bths->bts', wb.astype(jnp.float32), rel)
        adm = key_chunk[None, :] <= q_chunk[:, None]
        score = jnp.where(adm[None], score, -jnp.inf)
        _, sel = lax.top_k(score, top)
        valid = (sel // CHUNK) <= q_chunk[None, :, None]
        k_sel = gather(k, sel)
        v_sel = gather(v, sel)
        logits = jnp.einsum('bthd,btkd->bthk', qb, k_sel).astype(jnp.float32) * (hd ** -0.5)
        logits = jnp.where(valid[:, :, None, :], logits, -jnp.inf)
        p = jax.nn.softmax(logits, axis=-1).astype(v.dtype)
        return jnp.einsum('bthk,btkd->bthd', p, v_sel)

    to_blocks = lambda a: a.reshape(B, n_blk, Q_BLOCK, *a.shape[2:]).swapaxes(0, 1)
    out = lax.map(one_block, (to_blocks(q), to_blocks(q_idx), to_blocks(w_idx), jnp.arange(n_blk)))
    return out.swapaxes(0, 1).reshape(B, S, H * hd)


def setup_inputs(seed: int = 0) -> dict:
    key = jax.random.key(seed)
    ks = jax.random.split(key, 24)
    L = DEPTH
    nrm = lambda k, shape, s: jax.random.normal(k, shape, jnp.float32) * s
    gain = lambda k, n: 1.0 + nrm(k, (L, n), 0.02)
    return {
        "x": nrm(ks[0], (BATCH, SEQ, D_MODEL), 1.0),
        "ffn1_norm": gain(ks[1], D_MODEL),
        "ffn1_w1": nrm(ks[2], (L, D_MODEL, D_FF), D_MODEL ** -0.5),
        "ffn1_w3": nrm(ks[3], (L, D_MODEL, D_FF), D_MODEL ** -0.5),
        "ffn1_w2": nrm(ks[4], (L, D_FF, D_MODEL), D_FF ** -0.5),
        "mix_norm": gain(ks[5], D_MODEL),
        "w_in": nrm(ks[6], (L, D_MODEL, IN_WIDTH), D_MODEL ** -0.5),
        "gate_bias": nrm(ks[7], (L, 2 * D_MODEL), 0.02),
        "gmlp_v_norm": gain(ks[8], GMLP_WIDTH),
        "gmlp_ws": nrm(ks[9], (L, GMLP_GROUPS, GMLP_BLOCK, GMLP_BLOCK), GMLP_BLOCK ** -0.5),
        "gmlp_bs": 1.0 + nrm(ks[10], (L, GMLP_GROUPS, GMLP_BLOCK), 0.02),
        "q_norm": gain(ks[11], HEAD_DIM),
        "k_norm": gain(ks[12], HEAD_DIM),
        "idx_k_norm": gain(ks[13], IDX_DIM),
        "w_br_a": nrm(ks[14], (L, GMLP_WIDTH, D_MODEL), GMLP_WIDTH ** -0.5),
        "w_br_b": nrm(ks[15], (L, ATTN_WIDTH, D_MODEL), ATTN_WIDTH ** -0.5),
        "w_out": nrm(ks[16], (L, D_MODEL, D_MODEL), D_MODEL ** -0.5),
        "ffn2_norm": gain(ks[17], D_MODEL),
        "ffn2_w1": nrm(ks[18], (L, D_MODEL, D_FF), D_MODEL ** -0.5),
        "ffn2_w3": nrm(ks[19], (L, D_MODEL, D_FF), D_MODEL ** -0.5),
        "ffn2_w2": nrm(ks[20], (L, D_FF, D_MODEL), D_FF ** -0.5),
    }


def reference(x, ffn1_norm, ffn1_w1, ffn1_w3, ffn1_w2, mix_norm, w_in, gate_bias,
              gmlp_v_norm, gmlp_ws, gmlp_bs, q_norm, k_norm, idx_k_norm,
              w_br_a, w_br_b, w_out, ffn2_norm, ffn2_w1, ffn2_w3, ffn2_w2):
    B, S, _ = x.shape
    cos_a, sin_a = rope_tables(S, HEAD_DIM)
    cos_i, sin_i = rope_tables(S, IDX_DIM)
    split_at = [int(c) for c in np.cumsum(SPLITS)[:-1]]
    idx_scale = (IDX_HEADS ** -0.5) * (IDX_DIM ** -0.5)
    h = x
    for l in range(DEPTH):
        h = h + 0.5 * swiglu(rms_norm(h, ffn1_norm[l]), ffn1_w1[l], ffn1_w3[l], ffn1_w2[l])
        n = rms_norm(h, mix_norm[l])
        z = n @ w_in[l]
        (u_a, v_a, q, k, v, q_i, k_i, w_i, g_a, g_b) = jnp.split(z, split_at, axis=-1)
        y_a = gmlp_spatial_gating(u_a, v_a, gmlp_v_norm[l], gmlp_ws[l], gmlp_bs[l])
        q = apply_rope(rms_norm(q.reshape(B, S, ATTN_HEADS, HEAD_DIM), q_norm[l]), cos_a, sin_a)
        k = apply_rope(rms_norm(k, k_norm[l]), cos_a, sin_a)
        q_i = apply_rope(q_i.reshape(B, S, IDX_HEADS, IDX_DIM), cos_i, sin_i)
        k_i = apply_rope(rms_norm(k_i, idx_k_norm[l]), cos_i, sin_i)
        y_b = dsa_attention(q, k, v, q_i, k_i, w_i * idx_scale)
        gates = jax.nn.sigmoid((jnp.concatenate([g_a, g_b], axis=-1) + gate_bias[l]).astype(jnp.float32)).astype(h.dtype)
        gate_a, gate_b = jnp.split(gates, 2, axis=-1)
        m = gate_a * (y_a @ w_br_a[l]) + gate_b * (y_b @ w_br_b[l])
        h = h + m @ w_out[l]
        h = h + 0.5 * swiglu(rms_norm(h, ffn2_norm[l]), ffn2_w1[l], ffn2_w3[l], ffn2_w2[l])
    return h
```

```python
import numpy as np
from contextlib import ExitStack
import concourse.bass as bass
import concourse.mybir as mybir
from concourse.bass_utils import run_bass_kernel_spmd

F32 = mybir.dt.float32
BF16 = mybir.dt.bfloat16
ALU = mybir.AluOpType
AF = mybir.ActivationFunctionType
AX = mybir.AxisListType

D = 4096
DFF = 11008
NFF = DFF // 128
T = 256
EPS = 1e-6
C_U, C_V, C_Q, C_K, C_VV, C_QI, C_KI, C_WI, C_GA, C_GB = 0, 2048, 4096, 6144, 6272, 6400, 8448, 8512, 8544, 12640
IN_W = 16736
NEG = -1.0e30
TOPK = 256
SHAPE_OVR = {}
DEBUG = False
MIX = 9
WCACHE = True
WSC_COLS = 1000000
WSC_N = 4


class Op:
    __slots__ = ("eng", "fn", "deps", "signal", "idx", "chan", "target", "is_dma", "rk", "wk", "pos")

    def __init__(self, eng, fn):
        self.eng = eng
        self.fn = fn
        self.deps = []
        self.signal = False
        self.idx = None
        self.chan = None
        self.target = None
        self.is_dma = False


class Prog:
    ENGS = ("pe", "act", "dve", "pool", "sp")

    def __init__(self):
        self.ops = {e: [] for e in self.ENGS}
        self.last_w = {}
        self.readers = {}
        self.chan_tot = {}

    def _add(self, op, reads, writes):
        writes = list(writes) + [k for k in reads if k.startswith("ps")]
        reads = [k for k in reads if not k.startswith("ps")]
        deps = []
        for k in reads:
            w = self.last_w.get(k)
            if w is not None:
                deps.append(w)
        for k in writes:
            w = self.last_w.get(k)
            if w is not None:
                deps.append(w)
            lastr = {}
            for r in self.readers.get(k, ()):
                if r.is_dma:
                    deps.append(r)
                elif r.eng != op.eng or op.is_dma:
                    lastr[r.eng] = r
            deps.extend(lastr.values())
        seen = set()
        for d in deps:
            if id(d) in seen or d is op:
                continue
            seen.add(id(d))
            if d.eng == "pe" and op.eng == "pe":
                continue
            op.deps.append(d)
            if not d.is_dma:
                d.signal = True
        for k in reads:
            self.readers.setdefault(k, []).append(op)
        for k in writes:
            self.last_w[k] = op
            self.readers[k] = []
        op.rk, op.wk = list(reads), list(writes)
        op.pos = sum(len(v) for v in self.ops.values())
        self.ops[op.eng].append(op)
        return op

    def op(self, eng, fn, reads=(), writes=()):
        return self._add(Op(eng, fn), reads, writes)

    def dma(self, eng, chan, fn, reads=(), writes=()):
        o = Op(eng, fn)
        o.is_dma = True
        o.chan = chan
        self.chan_tot[chan] = self.chan_tot.get(chan, 0) + 16
        o.target = self.chan_tot[chan]
        return self._add(o, reads, writes)

    def barrier(self):
        lasts = {}
        for e in ("pe", "act", "dve"):
            real = [o for o in reversed(self.ops[e][-64:]) if o.fn is not None]
            if not real:
                real = [o for o in reversed(self.ops[e]) if o.fn is not None]
            if real:
                lasts[e] = real[0]
                real[0].signal = True
        chans = dict(self.chan_tot)
        for e in self.ENGS:
            o = Op(e, None)
            o.deps = [lasts[x] for x in lasts if x != e]
            o.chan = chans
            o.rk, o.wk, o.pos = [], [], -1
            self.ops[e].append(o)

    def emit(self, nc, es):
        engmap = {"pe": "tensor", "act": "scalar", "dve": "vector", "pool": "gpsimd", "sp": "sync"}
        sems = {e: es.enter_context(nc.semaphore("s_" + e)) for e in self.ENGS}
        csems = {c: es.enter_context(nc.semaphore("c_" + c)) for c in self.chan_tot}
        for e in self.ENGS:
            n = 0
            for o in self.ops[e]:
                if o.signal and not o.is_dma:
                    n += 1
                    o.idx = n
        block = es.enter_context(nc.Block())
        prog = self

        def run(ename):
            def body(eng):
                waited = {}
                for o in prog.ops[ename]:
                    for d in o.deps:
                        if d.is_dma:
                            key, val, sem = ("c", d.chan), d.target, csems[d.chan]
                        else:
                            key, val, sem = ("e", d.eng), d.idx, sems[d.eng]
                        if waited.get(key, 0) >= val:
                            continue
                        waited[key] = val
                        eng.wait_ge(sem, val)
                    if o.fn is None:
                        if isinstance(o.chan, dict):
                            for c_, tot in o.chan.items():
                                if waited.get(("c", c_), 0) < tot:
                                    waited[("c", c_)] = tot
                                    eng.wait_ge(csems[c_], tot)
                        assert not o.signal
                        continue
                    ins = o.fn(eng)
                    if o.is_dma:
                        ins.then_inc(csems[o.chan], 16)
                    elif o.signal:
                        ins.then_inc(sems[ename], 1)
            return body

        for e in self.ENGS:
            getattr(block, engmap[e])(run(e))


def build(NT_CTX=8, NT_OWN=8, STAGE=2, DBG=9):
    nc = bass.Bass("TRN2", target_bir_lowering=False)

    def di(name, shape, dt=F32):
        return nc.dram_tensor(name, SHAPE_OVR.get(name, shape), dt, kind="ExternalInput").ap()

    x_all = di("x_all", [4096, D])
    w1a, w3a, w2a = di("f1w1", [D, DFF]), di("f1w3", [D, DFF]), di("f1w2", [DFF, D])
    w1b, w3b, w2b = di("f2w1", [D, DFF]), di("f2w3", [D, DFF]), di("f2w2", [DFF, D])
    w_in = di("w_in", [D, IN_W])
    w_a, w_b, w_o = di("w_a", [2048, D]), di("w_b", [2048, D]), di("w_o", [D, D])
    ws_d = di("ws", [16, 128, 128])
    bs_d = di("bs", [1, 2048])
    gvn_d = di("gvn", [1, 2048])
    vecs_d = di("vecs", [128, 32 * 3 + 64 + 3])
    cmat_d = di("cmat", [128, 6 * 128])
    tabs_d = di("tabs", [4, 128, 4096])
    ctxb_d = di("ctxb", [128, 16])
    out_d = nc.dram_tensor("out", [2048, D], F32, kind="ExternalOutput").ap()
    h1s = nc.dram_tensor("h1s", [128, 32, 2048], F32).ap()
    nsd = nc.dram_tensor("nsd", [128, 32, 2048], BF16).ap()
    wscs = [nc.dram_tensor(f"wsc{i}", [128, WSC_COLS], BF16).ap() for i in range(WSC_N)]

    es = ExitStack()
    with es:
        sb = lambda name, shape, dt: es.enter_context(nc.sbuf_tensor(name, shape, dt))
        RA = sb("RA", [128, 8192], F32)
        RB = sb("RB", [128, 32, 256], BF16)
        RC = sb("RC", [128, 44 * 256], BF16)
        RD = [sb(f"RD{i}", [128, 8192], BF16) for i in range(3)]
        RE = sb("RE", [128, 12288], BF16)
        RF = sb("RF", [128, 12288], BF16)
        vecs = sb("vecs_sb", [128, 163], F32)
        cmat = sb("cmat_sb", [128, 768], F32)
        cbf = sb("cbf", [128, 256], BF16)
        ctxb = sb("ctxb_sb", [128, 16], F32)
        wsT = sb("wsT", [128, 16, 128], BF16)
        tabs = sb("tabs_sb", [128, 4, 256], F32)
        tmp = sb("tmpf", [128, 12, 256], F32)
        small = sb("small", [128, 64], F32)
        wabs = sb("wabs", [128, 2, 32], F32)
        wsgn = sb("wsgn", [128, 2, 32], F32)
        pst = [es.enter_context(nc.psum_tensor(f"ps{i}", [128, 512], F32)) for i in range(8)]

        hT = RA[:, :].rearrange("p (a b) -> p a b", b=256)
        vg = RA[:, 0:4096].rearrange("p (s c) -> p s c", c=2048)
        score = RA[:, 0:4096]
        yaT = RA[:, 4096:6144].bitcast(BF16).rearrange("p (a b) -> p a b", b=256)
        ybT = RA[:, 6144:8192].bitcast(BF16).rearrange("p (a b) -> p a b", b=256)
        xnT = RB
        hid = RC[:, :].rearrange("p (a b) -> p a b", b=256)
        xst = RC[:, 0:8192].bitcast(F32).rearrange("p (s c) -> p s c", c=2048)
        bs_bc = RC[:, 0:4096].bitcast(F32)
        gvn_bc = RC[:, 4096:8192].bitcast(F32)
        msk = RC[:, 0:4096]
        dg = RC[:, 4096:8192].rearrange("p (a b) -> p a b", b=128)
        rl_r = RC[:, 8192:9216].rearrange("p (a b) -> p a b", b=512)
        ex_r = RC[:, 9216:10240].rearrange("p (a b) -> p a b", b=512)
        pm_r = RC[:, 10240:11264].rearrange("p (a b) -> p a b", b=512)
        mT = RC[:, 0:8192].rearrange("p (a b) -> p a b", b=256)
        qT2 = RE[:, 0:4096].rearrange("p (s h t) -> p s h t", s=2, h=16)
        qiT = RE[:, 4096:8192].rearrange("p (a b) -> p a b", b=256)
        v_tm = RE[:, 8192:12288].rearrange("p (s c) -> p s c", c=2048)
        maskT = RE[:, 8192:12288].rearrange("p (a b) -> p a b", b=128)
        KT = RF[:, 0:4096]
        KIT = RF[:, 4096:8192]
        VV = RF[:, 8192:12288].rearrange("p (a b) -> p a b", b=128)
        ident_f, ones_f, ones64, R128, R64, dmask = [cmat[:, i * 128:(i + 1) * 128] for i in range(6)]
        ident_b, ones_b = cbf[:, 0:128], cbf[:, 128:256]
        g1, gm, g2 = vecs[:, 0:32], vecs[:, 32:64], vecs[:, 64:96]
        gateb = vecs[:, 96:160]
        qn, kn, ikn = vecs[:, 160:161], vecs[:, 161:162], vecs[:, 162:163]

        P = Prog()
        act = lambda fn, r, w: P.op("act", fn, r, w)
        dve = lambda fn, r, w: P.op("dve", fn, r, w)
        pe = lambda fn, r, w: P.op("pe", fn, r, w)

        P.dma("sp", "c0", lambda e: e.dma_start(out=vecs[:, :], in_=vecs_d[:, :]), writes=["vecs"])
        P.dma("sp", "c1", lambda e: e.dma_start(out=cmat[:, :], in_=cmat_d[:, :]), writes=["cmat"])
        P.dma("sp", "c2", lambda e: e.dma_start(out=ctxb[:, :], in_=ctxb_d[:, :]), writes=["ctxb"])
        dve(lambda e: e.tensor_copy(out=cbf[:, 0:128], in_=ident_f), ["cmat"], ["cbf"])
        dve(lambda e: e.memset(cbf[:, 128:256], 1.0), [], ["cbf1"])
        CK = ["vecs", "cmat", "cbf", "cbf1", "ctxb"]

        cnt = {"pp": 0, "pt": 0, "wc": 0, "t": 0}

        def next_pp():
            bank = 4 + cnt["pp"] % 4
            cnt["pp"] += 1
            return pst[bank][:, 0:256], f"ps{bank}"

        next_pt = next_pp

        def tmpf(i):
            return tmp[:, i, :], f"tmp{i}"

        wcache = {}
        wsc_next = [0, 0]

        def load_w(pieces, kcs, width=256):
            s_ = cnt["wc"] % 3
            cnt["wc"] += 1
            L = kcs * width
            view = RD[s_][:, 0:L].rearrange("p (a b) -> p a b", b=width)
            key = f"wc{s_}"
            ck = tuple((src.tensor.name, str(src.offset), str(src.ap), off, n) for (src, off, n) in pieces) + (kcs, width)
            if ck in wcache:
                wsc, o, bid = wcache[ck]
                P.dma("pool", f"wc{s_}", lambda e: e.dma_start(out=RD[s_][:, 0:L], in_=wsc[:, o:o + L]), reads=[f"wsc{bid}"], writes=[key])
                return view, [key]
            for i, (src, off, n) in enumerate(pieces):
                P.dma("pool", f"wc{s_}", lambda e, src=src, off=off, n=n: e.dma_start(
                    out=view[:, :, off:off + n], in_=src.rearrange("(a p) n -> p a n", p=128)), writes=[key])
            if WCACHE:
                if wsc_next[1] + L > WSC_COLS:
                    wsc_next[0] += 1
                    wsc_next[1] = 0
                wsc, o, bid = wscs[wsc_next[0]], wsc_next[1], len(wcache)
                wsc_next[1] += L
                wcache[ck] = (wsc, o, bid)
                P.dma("sp", f"wb{s_}", lambda e: e.dma_start(out=wsc[:, o:o + L], in_=RD[s_][:, 0:L]), reads=[key], writes=[f"wsc{bid}"])
            return view, [key]

        def rmsnorm_fm(gain, out_t, out_pref):
            st, stk = next_pp()
            for kc in range(32):
                sq, sqk = tmpf(kc % 2)
                act(lambda e, kc=kc, sq=sq: e.activation(out=sq, in_=hT[:, kc, :], func=AF.Square), [f"hT{kc}"], [sqk])
                pe(lambda e, kc=kc, sq=sq: e.matmul(st, lhsT=ones_f, rhs=sq, start=(kc == 0), stop=(kc == 31)),
                   [sqk, "cmat"], [stk])
            rs, rsk = tmpf(2)
            dve(lambda e: e.tensor_scalar(out=rs, in0=st, scalar1=1.0 / D, scalar2=EPS, op0=ALU.mult, op1=ALU.add), [stk], [rsk])
            act(lambda e: e.activation(out=rs, in_=rs, func=AF.Sqrt), [rsk], [rsk])
            dve(lambda e: e.reciprocal(out=rs, in_=rs), [rsk], [rsk])
            for kc in range(32):
                dve(lambda e, kc=kc: e.scalar_tensor_tensor(out=out_t[:, kc, :], in0=hT[:, kc, :], scalar=gain[:, kc:kc + 1],
                                                           in1=rs, op0=ALU.mult, op1=ALU.mult),
                    [f"hT{kc}", rsk, "vecs"], [f"{out_pref}{kc}"])

        def ffn(W1, W3, W2, gain):
            rmsnorm_fm(gain, xnT, "xn")
            first_dbg = dbg_state["n"] == 0
            dbg_state["n"] += 1
            if first_dbg:
                dump(tmp[:, 2, :], 0, ["tmp2"])
                dump(xnT[:, 0, :], 256, ["xn0"])
                dump(xnT[:, 31, :], 1280, ["xn31"])
            for (f0, f1) in ([(a_, b_) for (a_, b_) in ((0, min(44, NFF)), (min(44, NFF), NFF)) if b_ > a_] if DBG >= 2 else ()):
                f = f0
                while f < f1:
                    nf = min(2, f1 - f)
                    vA, kA = load_w([(W1[:, f * 128:(f + nf) * 128], 0, nf * 128)], 32)
                    vB, kB = load_w([(W3[:, f * 128:(f + nf) * 128], 0, nf * 128)], 32)
                    for j in range(nf):
                        psA, ka = next_pp()
                        psB, kb = next_pp()
                        for kc in range(32):
                            pe(lambda e, kc=kc, j=j, vA=vA, psA=psA: e.matmul(psA, lhsT=vA[:, kc, j * 128:(j + 1) * 128], rhs=xnT[:, kc, :],
                                                                             start=(kc == 0), stop=(kc == 31)), kA + [f"xn{kc}"], [ka])
                        for kc in range(32):
                            pe(lambda e, kc=kc, j=j, vB=vB, psB=psB: e.matmul(psB, lhsT=vB[:, kc, j * 128:(j + 1) * 128], rhs=xnT[:, kc, :],
                                                                             start=(kc == 0), stop=(kc == 31)), kB + [f"xn{kc}"], [kb])
                        sa, sak = tmpf(3 + (f + j) % 2)
                        act(lambda e, psA=psA, sa=sa: e.activation(out=sa, in_=psA, func=AF.Silu), [ka], [sak])
                        hi = f + j - f0
                        dve(lambda e, hi=hi, sa=sa, psB=psB: e.tensor_tensor(out=hid[:, hi, :], in0=sa, in1=psB, op=ALU.mult),
                            [sak, kb], [f"hid{hi}"])
                    f += nf
                if first_dbg and f0 == 0:
                    dump(hid[:, 0, :], 512, ["hid0"])
                    dump(hid[:, 1, :], 768, ["hid1"])
                for dq in range(8 if DBG >= 3 else 0):
                    ft = f0
                    while ft < f1:
                        nfc = min(16, f1 - ft)
                        vW, kW = load_w([(W2[ft * 128:(ft + nfc) * 128, dq * 512:(dq + 1) * 512], 0, 512)], nfc, width=512)
                        for fi in range(nfc):
                            fg = ft + fi
                            for d in range(4):
                                acc = pst[d][:, 0:256]
                                pe(lambda e, acc=acc, vW=vW, fi=fi, d=d, hi2=fg - f0, st_=(fg == f0), sp_=(fg == f1 - 1): e.matmul(
                                    acc, lhsT=vW[:, fi, d * 128:(d + 1) * 128], rhs=hid[:, hi2, :], start=st_, stop=sp_),
                                   kW + [f"hid{fg - f0}"], [f"ps{d}"])
                        ft += nfc
                    for d in range(4):
                        acc = pst[d][:, 0:256]
                        c = dq * 4 + d
                        dve(lambda e, acc=acc, c=c: e.scalar_tensor_tensor(out=hT[:, c, :], in0=acc, scalar=0.5, in1=hT[:, c, :],
                                                                         op0=ALU.mult, op1=ALU.add),
                            [f"ps{d}", f"hT{c}"], [f"hT{c}"])

        def normrope(ps, psk, w_, gain, ones_ap, Dn, R_ap, ti_cos, ti_sin, out_ap, out_keys):
            r = cnt["t"] % 2
            cnt["t"] += 1
            qr, qrk = tmpf(5 + r)
            act(lambda e: e.copy(out=qr[0:w_, :], in_=ps), [psk], [qrk])
            if gain is not None:
                sq, sqk = tmpf(7 + r)
                act(lambda e: e.activation(out=sq[0:w_, :], in_=ps, func=AF.Square), [psk], [sqk])
                st, stk = next_pp()
                pe(lambda e: e.matmul(st[0:w_, :], lhsT=ones_ap[0:w_, 0:w_], rhs=sq[0:w_, :], start=True, stop=True), [sqk, "cmat"], [stk])
                rs, rsk = tmpf(9 + r)
                dve(lambda e: e.tensor_scalar(out=rs[0:w_, :], in0=st[0:w_, :], scalar1=1.0 / Dn, scalar2=EPS, op0=ALU.mult, op1=ALU.add), [stk], [rsk])
                act(lambda e: e.activation(out=rs[0:w_, :], in_=rs[0:w_, :], func=AF.Sqrt), [rsk], [rsk])
                dve(lambda e: e.reciprocal(out=rs[0:w_, :], in_=rs[0:w_, :]), [rsk], [rsk])
                dve(lambda e: e.scalar_tensor_tensor(out=qr[0:w_, :], in0=qr[0:w_, :], scalar=gain[0:w_, :], in1=rs[0:w_, :],
                                                     op0=ALU.mult, op1=ALU.mult), [qrk, rsk, "vecs"], [qrk])
            sw, swk = next_pp()
            pe(lambda e: e.matmul(sw[0:w_, :], lhsT=R_ap[0:w_, 0:w_], rhs=qr[0:w_, :], start=True, stop=True), [qrk, "cmat"], [swk])
            t1, t1k = tmpf(7 + r)
            dve(lambda e: e.tensor_tensor(out=t1[0:w_, :], in0=qr[0:w_, :], in1=tabs[0:w_, ti_cos, :], op=ALU.mult), [qrk, "tabs"], [t1k])
            t2, t2k = tmpf(9 + r)
            dve(lambda e: e.tensor_tensor(out=t2[0:w_, :], in0=sw[0:w_, :], in1=tabs[0:w_, ti_sin, :], op=ALU.mult), [swk, "tabs"], [t2k])
            if out_ap.ndim == 2:
                dve(lambda e: e.tensor_tensor(out=out_ap, in0=t1[0:w_, :], in1=t2[0:w_, :], op=ALU.add), [t1k, t2k], out_keys)
            else:
                dve(lambda e: e.tensor_tensor(out=out_ap, in0=t1[0:w_, :].rearrange("p (s t) -> p s t", s=2),
                                              in1=t2[0:w_, :].rearrange("p (s t) -> p s t", s=2), op=ALU.add), [t1k, t2k], out_keys)

        def load_x_tile(r0):
            for half in range(2):
                for sub in range(2):
                    P.dma("sp", f"xs{sub}", lambda e, sub=sub, half=half: e.dma_start(
                        out=xst[:, sub, :], in_=x_all[r0 + sub * 128:r0 + (sub + 1) * 128, half * 2048:(half + 1) * 2048]),
                        writes=[f"xs{sub}"])
                for kcl in range(16):
                    kc = half * 16 + kcl
                    pt, ptk = next_pt()
                    for sub in range(2):
                        pe(lambda e, pt=pt, sub=sub, kcl=kcl: e.transpose(pt[:, sub * 128:(sub + 1) * 128], xst[:, sub, kcl * 128:(kcl + 1) * 128], ident_f),
                           [f"xs{sub}", "cmat"], [ptk])
                    if kc % 2 == 0:
                        act(lambda e, pt=pt, kc=kc: e.copy(out=hT[:, kc, :], in_=pt), [ptk], [f"hT{kc}"])
                    else:
                        dve(lambda e, pt=pt, kc=kc: e.tensor_copy(out=hT[:, kc, :], in_=pt), [ptk], [f"hT{kc}"])

        def store_out_tile(r0):
            P.barrier()
            for half in range(2):
                for kcl in range(16):
                    kc = half * 16 + kcl
                    for sub in range(2):
                        pt, ptk = next_pt()
                        pe(lambda e, pt=pt, sub=sub, kc=kc: e.transpose(pt[:, 0:128], hT[:, kc, sub * 128:(sub + 1) * 128], ident_f),
                           [f"hT{kc}", "cmat"], [ptk])
                        if sub == 0:
                            act(lambda e, pt=pt, kcl=kcl, sub=sub: e.copy(out=xst[:, sub, kcl * 128:(kcl + 1) * 128], in_=pt[:, 0:128]), [ptk], [f"xs{sub}"])
                        else:
                            dve(lambda e, pt=pt, kcl=kcl, sub=sub: e.tensor_copy(out=xst[:, sub, kcl * 128:(kcl + 1) * 128], in_=pt[:, 0:128]), [ptk], [f"xs{sub}"])
                for sub in range(2):
                    P.dma("sp", f"xs{sub}", lambda e, sub=sub, half=half: e.dma_start(
                        out=out_d[r0 + sub * 128:r0 + (sub + 1) * 128, half * 2048:(half + 1) * 2048], in_=xst[:, sub, :]),
                        reads=[f"xs{sub}"], writes=[f"outd{r0}_{half}_{sub}"])
                    OUTK.append(f"outd{r0}_{half}_{sub}")
            P.barrier()

        OUTK = []
        dbg_state = {"n": 0}
        dbg_d = nc.dram_tensor("dbg", [128, 8192], F32, kind="ExternalOutput").ap() if DEBUG else None

        def dump(ap, c0, reads):
            if not DEBUG:
                return
            n = ap.shape[-1]
            k = f"dbg{c0}"
            P.dma("pool", "dbgc", lambda e: e.dma_start(out=dbg_d[0:ap.shape[0], c0:c0 + n], in_=ap), reads=reads, writes=[k])
            OUTK.append(k)

        def load_tabs(c0):
            P.dma("sp", "tabs", lambda e: e.dma_start(out=tabs[:, :, :], in_=tabs_d[:, :, c0:c0 + 256].rearrange("f p t -> p f t")), writes=["tabs"])

        def phase1_tile(ti, own):
            r0 = ti * 256
            P.barrier()
            load_x_tile(r0)
            if DBG >= 1:
                ffn(w1a, w3a, w2a, g1)
            if STAGE == 1:
                return
            if own:
                oc = (ti - 8) * 256
                P.dma("sp", "h1o", lambda e: e.dma_start(out=h1s[:, :, oc:oc + 256], in_=hT[:, :, :]),
                      reads=[f"hT{k}" for k in range(32)], writes=[f"h1s{ti}"])
            rmsnorm_fm(gm, xnT, "xn")
            if own:
                P.dma("sp", "nso", lambda e: e.dma_start(out=nsd[:, :, oc:oc + 256], in_=xnT[:, :, :]),
                      reads=[f"xn{k}" for k in range(32)], writes=[f"ns{ti}"])
            load_tabs(r0)
            vK, kK = load_w([(w_in[:, C_K:C_K + 256], 0, 256)], 32)
            psk_, pkk = next_pp()
            for kc in range(32):
                pe(lambda e, kc=kc: e.matmul(psk_, lhsT=vK[:, kc, 0:128], rhs=xnT[:, kc, :], start=(kc == 0), stop=(kc == 31)),
                   kK + [f"xn{kc}"], [pkk])
            normrope(psk_, pkk, 128, kn, ones_f, 128.0, R128, 0, 1, KT[:, r0:r0 + 256], [f"KT{ti}"])
            psv, pvk = next_pp()
            for sub in range(2):
                for kc in range(32):
                    pe(lambda e, kc=kc, sub=sub: e.matmul(psv[:, sub * 128:(sub + 1) * 128], lhsT=xnT[:, kc, sub * 128:(sub + 1) * 128],
                                                          rhs=vK[:, kc, 128:256], start=(kc == 0), stop=(kc == 31)),
                       kK + [f"xn{kc}"], [pvk])
            act(lambda e: e.copy(out=VV[:, 2 * ti:2 * ti + 2, :], in_=psv.rearrange("p (s c) -> p s c", s=2)), [pvk], [f"VV{ti}"])
            vI, kI = load_w([(w_in[:, C_KI:C_KI + 64], 0, 64), (w_in[:, C_KI:C_KI + 64], 64, 64)], 32)
            psi, pik = next_pp()
            for kc in range(32):
                pe(lambda e, kc=kc: e.matmul(psi, lhsT=vI[:, kc, 0:128], rhs=xnT[:, kc, :], start=(kc == 0), stop=(kc == 31)),
                   kI + [f"xn{kc}"], [pik])
            normrope(psi, pik, 128, ikn, ones64, 64.0, R64, 2, 3, KIT[:, r0:r0 + 256], [f"KIT{ti}"])

        def mixer_tile(i):
            ti = 8 + i
            oc = i * 256
            r0 = ti * 256
            P.barrier()
            P.dma("sp", "nsi", lambda e: e.dma_start(out=xnT[:, :, :], in_=nsd[:, :, oc:oc + 256]),
                  reads=[f"ns{ti}"], writes=[f"xn{k}" for k in range(32)])
            load_tabs(r0)
            P.dma("sp", "bsb", lambda e: e.dma_start(out=bs_bc, in_=bs_d[0:1, :].to_broadcast([128, 2048])), writes=["bs_bc"])
            P.dma("sp", "gvb", lambda e: e.dma_start(out=gvn_bc, in_=gvn_d[0:1, :].to_broadcast([128, 2048])), writes=["gvn_bc"])
            XN = [f"xn{k}" for k in range(32)]

            def bail():
                P.barrier()
                P.dma("sp", "h1i", lambda e: e.dma_start(out=hT[:, :, :], in_=h1s[:, :, oc:oc + 256]), reads=[f"h1s{ti}"], writes=[f"hT{k}" for k in range(32)])
                P.barrier()

            if MIX < 1:
                return bail()
            for cb in range(8):
                vW, kW = load_w([(w_in[:, C_V + cb * 256:C_V + (cb + 1) * 256], 0, 256)], 32)
                for sub in range(2):
                    ps, pk = next_pp()
                    for kc in range(32):
                        pe(lambda e, kc=kc, sub=sub, ps=ps, vW=vW: e.matmul(ps, lhsT=xnT[:, kc, sub * 128:(sub + 1) * 128], rhs=vW[:, kc, :],
                                                                           start=(kc == 0), stop=(kc == 31)), kW + [f"xn{kc}"], [pk])
                    act(lambda e, ps=ps, sub=sub, cb=cb: e.activation(out=vg[:, sub, cb * 256:(cb + 1) * 256], in_=ps, func=AF.Gelu), [pk], [f"vg{sub}_{cb}"])
            for sub in range(2):
                VG = [f"vg{sub}_{cb}" for cb in range(8)]
                dve(lambda e, sub=sub: e.memset(small[:, sub:sub + 1], 0.0), [], [f"ssv{sub}"])
                act(lambda e, sub=sub: e.activation(out=RA[:, 4096:6144], in_=vg[:, sub, :], func=AF.Square,
                                                   accum_out=small[:, sub:sub + 1]), VG + [f"ssv{sub}"], [f"ssv{sub}"] + [f"ya{g}" for g in range(16)])
                dve(lambda e, sub=sub: e.tensor_scalar(out=small[:, 2 + sub:3 + sub], in0=small[:, sub:sub + 1], scalar1=1.0 / 2048, scalar2=EPS,
                                                      op0=ALU.mult, op1=ALU.add), [f"ssv{sub}"], [f"rsv{sub}"])
                act(lambda e, sub=sub: e.activation(out=small[:, 2 + sub:3 + sub], in_=small[:, 2 + sub:3 + sub], func=AF.Sqrt), [f"rsv{sub}"], [f"rsv{sub}"])
                dve(lambda e, sub=sub: e.reciprocal(out=small[:, 2 + sub:3 + sub], in_=small[:, 2 + sub:3 + sub]), [f"rsv{sub}"], [f"rsv{sub}"])
                dve(lambda e, sub=sub: e.scalar_tensor_tensor(out=v_tm[:, sub, :], in0=vg[:, sub, :], scalar=small[:, 2 + sub:3 + sub], in1=gvn_bc,
                                                             op0=ALU.mult, op1=ALU.mult), VG + [f"rsv{sub}", "gvn_bc"], [f"vtm{sub}"])
            for cb in range(8):
                vW, kW = load_w([(w_in[:, C_U + cb * 256:C_U + (cb + 1) * 256], 0, 256)], 32)
                for j in range(2):
                    g = cb * 2 + j
                    ps, pk = next_pp()
                    for kc in range(32):
                        pe(lambda e, kc=kc, j=j, ps=ps, vW=vW: e.matmul(ps, lhsT=vW[:, kc, j * 128:(j + 1) * 128], rhs=xnT[:, kc, :],
                                                                       start=(kc == 0), stop=(kc == 31)), kW + [f"xn{kc}"], [pk])
                    gu, guk = tmpf(3 + g % 2)
                    act(lambda e, ps=ps, gu=gu: e.activation(out=gu, in_=ps, func=AF.Gelu), [pk], [guk])
                    pm_, pmk = next_pt()
                    for sub in range(2):
                        pe(lambda e, pm_=pm_, sub=sub, g=g: e.matmul(pm_[:, sub * 128:(sub + 1) * 128], lhsT=v_tm[:, sub, g * 128:(g + 1) * 128],
                                                                    rhs=wsT[:, g, :], start=True, stop=True), [f"vtm{sub}", "wsT"], [pmk])
                    tt, ttk = tmpf(5 + g % 2)
                    dve(lambda e, pm_=pm_, tt=tt, g=g: e.tensor_tensor(out=tt.rearrange("p (s t) -> p s t", s=2), in0=pm_.rearrange("p (s t) -> p s t", s=2),
                                                                      in1=bs_bc[:, g * 128:(g + 1) * 128].unsqueeze(1).to_broadcast([128, 2, 128]), op=ALU.add),
                        [pmk, "bs_bc"], [ttk])
                    dve(lambda e, tt=tt, gu=gu, g=g: e.tensor_tensor(out=yaT[:, g, :], in0=tt, in1=gu, op=ALU.mult), [ttk, guk], [f"ya{g}"])
            if MIX < 2:
                return bail()
            for cb in range(8):
                vW, kW = load_w([(w_in[:, C_Q + cb * 256:C_Q + (cb + 1) * 256], 0, 256)], 32)
                for j in range(2):
                    h = cb * 2 + j
                    ps, pk = next_pp()
                    for kc in range(32):
                        pe(lambda e, kc=kc, j=j, ps=ps, vW=vW: e.matmul(ps, lhsT=vW[:, kc, j * 128:(j + 1) * 128], rhs=xnT[:, kc, :],
                                                                       start=(kc == 0), stop=(kc == 31)), kW + [f"xn{kc}"], [pk])
                    normrope(ps, pk, 128, qn, ones_f, 128.0, R128, 0, 1, qT2[:, :, h, :], [f"qT{h}"])
            for cb in range(8):
                vW, kW = load_w([(w_in[:, C_QI + cb * 256:C_QI + (cb + 1) * 256], 0, 256)], 32)
                for j in range(2):
                    c = cb * 2 + j
                    ps, pk = next_pp()
                    for kc in range(32):
                        pe(lambda e, kc=kc, j=j, ps=ps, vW=vW: e.matmul(ps, lhsT=vW[:, kc, j * 128:(j + 1) * 128], rhs=xnT[:, kc, :],
                                                                       start=(kc == 0), stop=(kc == 31)), kW + [f"xn{kc}"], [pk])
                    normrope(ps, pk, 128, None, None, 0, R64, 2, 3, qiT[:, c, :], [f"qi{c}"])
            vW, kW = load_w([(w_in[:, C_WI:C_WI + 32], 0, 32)], 32)
            psw, pwk = next_pp()
            for sub in range(2):
                for kc in range(32):
                    pe(lambda e, kc=kc, sub=sub, vW=vW: e.matmul(psw[:, sub * 32:(sub + 1) * 32], lhsT=xnT[:, kc, sub * 128:(sub + 1) * 128], rhs=vW[:, kc, 0:32],
                                                          start=(kc == 0), stop=(kc == 31)), kW + [f"xn{kc}"], [pwk])
            idx_scale = (32 ** -0.5) * (64 ** -0.5)
            dve(lambda e: e.tensor_scalar(out=wsgn[:, :, :], in0=psw[:, 0:64].rearrange("p (s h) -> p s h", s=2), scalar1=0.0, scalar2=2.0,
                                          op0=ALU.is_ge, op1=ALU.mult), [pwk], ["wsgn"])
            dve(lambda e: e.tensor_scalar(out=wsgn[:, :, :], in0=wsgn[:, :, :], scalar1=-1.0, scalar2=None, op0=ALU.add), ["wsgn"], ["wsgn"])
            dve(lambda e: e.scalar_tensor_tensor(out=wabs[:, :, :], in0=psw[:, 0:64].rearrange("p (s h) -> p s h", s=2), scalar=idx_scale,
                                                 in1=wsgn[:, :, :], op0=ALU.mult, op1=ALU.mult), [pwk, "wsgn"], ["wabs"])
            P.barrier()
            if MIX < 2.1:
                return bail()
            def qblock(sub):
                j = 2 * i + sub
                nb = j + 1
                nk = 2 * nb * 128
                for h in range(32):
                    dve(lambda e, h=h, sub=sub: e.tensor_scalar(out=dg[:, h, :], in0=ident_b, scalar1=wsgn[:, sub, h:h + 1], scalar2=None, op0=ALU.mult),
                        ["wsgn", "cbf"], [f"dg{h}"])
                parts = []
                for part in range(2):
                    for b0 in range(0, nb, 4):
                        nbb = min(4, nb - b0)
                        parts.append((part * 2048 + b0 * 128, part * nb * 128 + b0 * 128, nbb * 128))
                for pi_, (kcol, scol, ncol) in enumerate(parts):
                    pS, pSk = pst[6][:, 0:ncol], ["ps6"]
                    for h in range(32):
                        c, hf = h // 2, h % 2
                        pI = pst[4 + h % 2][:, 0:ncol]
                        pIk = [f"ps{4 + h % 2}"]
                        pe(lambda e, pI=pI, c=c, hf=hf, kcol=kcol, ncol=ncol, sub=sub: e.matmul(
                            pI, lhsT=qiT[hf * 64:(hf + 1) * 64, c, sub * 128:(sub + 1) * 128], rhs=KIT[hf * 64:(hf + 1) * 64, kcol:kcol + ncol],
                            start=True, stop=True), [f"qi{c}", "KITall"], pIk)
                        rl = rl_r[:, h % 2, 0:ncol]
                        act(lambda e, pI=pI, rl=rl, h=h, sub=sub: e.activation(out=rl, in_=pI, func=AF.Relu, scale=wabs[:, sub, h:h + 1]),
                            pIk + ["wabs"], [f"rl{h % 2}"])
                        pe(lambda e, pS=pS, rl=rl, h=h: e.matmul(pS, lhsT=dg[:, h, :], rhs=rl, start=(h == 0), stop=(h == 31)),
                           [f"rl{h % 2}", f"dg{h}"], pSk)
                    act(lambda e, pS=pS, scol=scol, ncol=ncol: e.copy(out=score[:, scol:scol + ncol], in_=pS), pSk, [f"sc{pi_}"])
                if MIX < 2.3:
                    return
                SC = [f"sc{p_}" for p_ in range(len(parts))]
                sm = lambda a: small[:, a:a + 1]
                dve(lambda e: e.tensor_reduce(out=sm(8), in_=score[:, 0:nk], axis=AX.X, op=ALU.max), SC, ["s8"])
                dve(lambda e: e.tensor_reduce(out=sm(9), in_=score[:, 0:nk], axis=AX.X, op=ALU.min), SC, ["s9"])
                dve(lambda e: e.scalar_tensor_tensor(out=sm(10), in0=sm(9), scalar=-1.0, in1=sm(8), op0=ALU.mult, op1=ALU.max), ["s8", "s9"], ["s10"])
                dve(lambda e: e.tensor_scalar(out=sm(11), in0=sm(10), scalar1=-1.001, scalar2=-1e-6, op0=ALU.mult, op1=ALU.add), ["s10"], ["lo"])
                dve(lambda e: e.tensor_scalar(out=sm(12), in0=sm(11), scalar1=-2.0, scalar2=None, op0=ALU.mult), ["lo"], ["w0"])
                cl = (nb - 1) * 128
                dve(lambda e: e.tensor_scalar(out=score[:, cl:cl + 128], in0=score[:, cl:cl + 128], scalar1=ctxb[:, j:j + 1], scalar2=None, op0=ALU.add),
                    SC + ["ctxb"], ["scb"])
                ol = nb * 128 + (nb - 1) * 128
                dve(lambda e: e.tensor_tensor(out=score[:, ol:ol + 128], in0=score[:, ol:ol + 128], in1=dmask, op=ALU.add), SC + ["cmat"], ["scc"])
                SCALL = SC + ["scb", "scc"]
                for it in range(20):
                    fac = 2.0 ** (-(it + 1))
                    dve(lambda e, fac=fac: e.scalar_tensor_tensor(out=sm(13), in0=sm(12), scalar=fac, in1=sm(11), op0=ALU.mult, op1=ALU.add), ["w0", "lo"], ["mid"])
                    dve(lambda e: e.memset(sm(14), 0.0), [], ["cnt"])
                    dve(lambda e: e.tensor_scalar(out=msk[:, 0:nk], in0=score[:, 0:nk], scalar1=sm(13), scalar2=0.0, op0=ALU.is_ge, op1=ALU.add,
                                                  accum_out=sm(14)), SCALL + ["mid", "cnt"], ["cnt", "msk"])
                    dve(lambda e: e.tensor_scalar(out=sm(15), in0=sm(14), scalar1=TOPK - 0.5, scalar2=None, op0=ALU.is_ge), ["cnt"], ["sel"])
                    dve(lambda e, fac=fac: e.scalar_tensor_tensor(out=sm(16), in0=sm(15), scalar=fac, in1=sm(12), op0=ALU.mult, op1=ALU.mult), ["sel", "w0"], ["stp"])
                    dve(lambda e: e.tensor_tensor(out=sm(11), in0=sm(11), in1=sm(16), op=ALU.add), ["lo", "stp"], ["lo"])
                dve(lambda e: e.tensor_scalar(out=msk[:, 0:nk], in0=score[:, 0:nk], scalar1=sm(11), scalar2=None, op0=ALU.is_ge), SCALL + ["lo"], ["msk"])
                if MIX < 2.5:
                    return
                ptb = pst[7][:, 0:256].bitcast(BF16)
                for kb in range(2 * nb):
                    q4 = kb % 4
                    pe(lambda e, kb=kb, q4=q4: e.transpose(ptb[:, q4 * 128:(q4 + 1) * 128], msk[:, kb * 128:(kb + 1) * 128], ident_b), ["msk", "cbf"], ["ps7"])
                    act(lambda e, kb=kb, q4=q4: e.copy(out=maskT[:, kb, :], in_=ptb[:, q4 * 128:(q4 + 1) * 128]), ["ps7"], [f"mT{kb}"])
                if MIX < 2.7:
                    return
                for hg in range(4):
                    pO, pOk = pst[0][:, :], ["ps0"]
                    pD, pDk = pst[1][:, :], ["ps1"]
                    for kb in range(2 * nb):
                        part, b = (0, kb) if kb < nb else (1, kb - nb)
                        kcol = part * 2048 + b * 128
                        vblk = part * 16 + b
                        pL = pst[4 + kb % 2][:, :]
                        pLk = [f"ps{4 + kb % 2}"]
                        pe(lambda e, pL=pL, kcol=kcol, hg=hg, sub=sub: e.matmul(pL, lhsT=KT[:, kcol:kcol + 128],
                                                                               rhs=RE[:, sub * 2048 + hg * 512:sub * 2048 + (hg + 1) * 512], start=True, stop=True),
                           ["KTall"] + [f"qT{hh}" for hh in range(hg * 4, hg * 4 + 4)], pLk)
                        ex = ex_r[:, kb % 2, :]
                        act(lambda e, pL=pL, ex=ex: e.activation(out=ex, in_=pL, func=AF.Exp, scale=128.0 ** -0.5), pLk, [f"ex{kb % 2}"])
                        pm = pm_r[:, kb % 2, :]
                        dve(lambda e, pm=pm, ex=ex, kb=kb: e.tensor_tensor(out=pm.rearrange("p (h t) -> p h t", h=4), in0=ex.rearrange("p (h t) -> p h t", h=4),
                                                                          in1=maskT[:, kb, :].unsqueeze(1).to_broadcast([128, 4, 128]), op=ALU.mult),
                            [f"ex{kb % 2}", f"mT{kb}"], [f"pm{kb % 2}"])
                        pe(lambda e, pm=pm, vblk=vblk, kb=kb: e.matmul(pO, lhsT=VV[:, vblk, :], rhs=pm, start=(kb == 0), stop=(kb == 2 * nb - 1)),
                           [f"pm{kb % 2}", "VVall"], pOk)
                        pe(lambda e, pm=pm, kb=kb: e.matmul(pD, lhsT=ones_b, rhs=pm, start=(kb == 0), stop=(kb == 2 * nb - 1)),
                           [f"pm{kb % 2}", "cbf1"], pDk)
                    rc0, rc0k = tmpf(7)
                    rc1, rc1k = tmpf(8)
                    rec = tmp[:, 7:9, :].rearrange("p a b -> p (a b)")
                    dve(lambda e: e.reciprocal(out=rec, in_=pD), pDk, [rc0k, rc1k])
                    dve(lambda e, hg=hg, sub=sub: e.tensor_tensor(out=ybT[:, hg * 4:(hg + 1) * 4, sub * 128:(sub + 1) * 128],
                                                                  in0=pO.rearrange("p (h t) -> p h t", h=4), in1=rec.rearrange("p (h t) -> p h t", h=4), op=ALU.mult),
                        pOk + [rc0k, rc1k], [f"yb{hg}_{sub}"])
            for sub_ in range(2):
                qblock(sub_)
            P.barrier()
            if MIX < 4:
                return bail()
            for cb in range(16):
                vGa, kGa = load_w([(w_in[:, C_GA + cb * 256:C_GA + (cb + 1) * 256], 0, 256)], 32)
                gts = []
                for j in range(2):
                    dch = cb * 2 + j
                    ps, pk = next_pp()
                    for kc in range(32):
                        pe(lambda e, kc=kc, j=j, ps=ps, vGa=vGa: e.matmul(ps, lhsT=vGa[:, kc, j * 128:(j + 1) * 128], rhs=xnT[:, kc, :],
                                                                         start=(kc == 0), stop=(kc == 31)), kGa + [f"xn{kc}"], [pk])
                    ga, gak = tmpf(3 + j)
                    act(lambda e, ps=ps, ga=ga, dch=dch: e.activation(out=ga, in_=ps, func=AF.Sigmoid, bias=gateb[:, dch:dch + 1]), [pk, "vecs"], [gak])
                vA, kA = load_w([(w_a[:, cb * 256:(cb + 1) * 256], 0, 256)], 16)
                for j in range(2):
                    ga, gak = tmpf(3 + j)
                    ps2, pk2 = next_pp()
                    for g in range(16):
                        pe(lambda e, g=g, j=j, ps2=ps2, vA=vA: e.matmul(ps2, lhsT=vA[:, g, j * 128:(j + 1) * 128], rhs=yaT[:, g, :],
                                                                       start=(g == 0), stop=(g == 15)), kA + [f"ya{g}"], [pk2])
                    dve(lambda e, ps2=ps2, ga=ga: e.tensor_tensor(out=ga, in0=ps2, in1=ga, op=ALU.mult), [pk2, gak], [gak])
                vGb, kGb = load_w([(w_in[:, C_GB + cb * 256:C_GB + (cb + 1) * 256], 0, 256)], 32)
                for j in range(2):
                    dch = cb * 2 + j
                    ps3, pk3 = next_pp()
                    for kc in range(32):
                        pe(lambda e, kc=kc, j=j, ps3=ps3, vGb=vGb: e.matmul(ps3, lhsT=vGb[:, kc, j * 128:(j + 1) * 128], rhs=xnT[:, kc, :],
                                                                           start=(kc == 0), stop=(kc == 31)), kGb + [f"xn{kc}"], [pk3])
                    gb, gbk = tmpf(5 + j)
                    act(lambda e, ps3=ps3, gb=gb, dch=dch: e.activation(out=gb, in_=ps3, func=AF.Sigmoid, bias=gateb[:, 32 + dch:33 + dch]), [pk3, "vecs"], [gbk])
                vB, kB = load_w([(w_b[:, cb * 256:(cb + 1) * 256], 0, 256)], 16)
                for j in range(2):
                    dch = cb * 2 + j
                    ga, gak = tmpf(3 + j)
                    gb, gbk = tmpf(5 + j)
                    ps4, pk4 = next_pp()
                    for hh in range(16):
                        pe(lambda e, hh=hh, j=j, ps4=ps4, vB=vB: e.matmul(ps4, lhsT=vB[:, hh, j * 128:(j + 1) * 128], rhs=ybT[:, hh, :],
                                                                         start=(hh == 0), stop=(hh == 15)),
                           kB + [f"yb{hh // 4}_0", f"yb{hh // 4}_1"], [pk4])
                    dve(lambda e, ps4=ps4, gb=gb: e.tensor_tensor(out=gb, in0=ps4, in1=gb, op=ALU.mult), [pk4, gbk], [gbk])
                    dve(lambda e, dch=dch, ga=ga, gb=gb: e.tensor_tensor(out=mT[:, dch, :], in0=ga, in1=gb, op=ALU.add), [gak, gbk], [f"m{dch}"])
            P.barrier()
            if MIX < 5:
                return bail()
            P.dma("sp", "h1i", lambda e: e.dma_start(out=hT[:, :, :], in_=h1s[:, :, oc:oc + 256]), reads=[f"h1s{ti}"], writes=[f"hT{k}" for k in range(32)])
            for cb in range(16):
                vO, kO = load_w([(w_o[:, cb * 256:(cb + 1) * 256], 0, 256)], 32)
                for j in range(2):
                    dch = cb * 2 + j
                    ps, pk = next_pp()
                    for kc in range(32):
                        pe(lambda e, kc=kc, j=j, ps=ps, vO=vO: e.matmul(ps, lhsT=vO[:, kc, j * 128:(j + 1) * 128], rhs=mT[:, kc, :],
                                                                       start=(kc == 0), stop=(kc == 31)), kO + [f"m{kc}"], [pk])
                    dve(lambda e, ps=ps, dch=dch: e.tensor_tensor(out=hT[:, dch, :], in0=hT[:, dch, :], in1=ps, op=ALU.add), [pk, f"hT{dch}"], [f"hT{dch}"])
            P.barrier()


        if STAGE == 1:
            for i in range(NT_OWN):
                phase1_tile(8 + i, True)
                store_out_tile(i * 256)
        else:
            for g in range(16):
                wst, wstk = tmpf(g % 2)
                P.dma("sp", f"wsl{g % 2}", lambda e, g=g, wst=wst: e.dma_start(out=wst[:, 0:128], in_=ws_d[g, :, :]), writes=[wstk])
                pt, ptk = next_pt()
                pe(lambda e, pt=pt, wst=wst: e.transpose(pt[:, 0:128], wst[:, 0:128], ident_f), [wstk, "cmat"], [ptk])
                dve(lambda e, pt=pt, g=g: e.tensor_copy(out=wsT[:, g, :], in_=pt[:, 0:128]), [ptk], ["wsT"])
                dve(lambda e, g=g: e.memset(wsT[64:128, g, 0:64], 0.0), ["wsT"], ["wsT"])
            for ti in range(NT_CTX):
                phase1_tile(ti, False)
            for i in range(NT_OWN):
                phase1_tile(8 + i, True)
            P.barrier()
            for i in range(NT_OWN):
                mixer_tile(i)
                ffn(w1b, w3b, w2b, g2)
                store_out_tile(i * 256)
        P.op("sp", None, reads=OUTK, writes=[])
        P.emit(nc, es)
        global _LASTP
        _LASTP = P
    return nc


def _owner(p):
    return (0, 1, 1, 0)[p % 4]


def _core_blocks(r):
    own = [p for p in range(32) if _owner(p) == r]
    ctx = [p for p in range(32) if _owner(p) != r]
    return ctx, own


def _rope_tabs(pos):
    pos = pos.astype(np.float32)
    out = np.zeros((4, 128, pos.shape[0]), np.float32)
    invA = (np.float32(10000.0) ** (-np.arange(0, 128, 2, dtype=np.float32) / np.float32(128))).astype(np.float32)
    angA = (pos[:, None] * invA[None, :]).astype(np.float32)
    cA, sA = np.cos(angA).astype(np.float32), np.sin(angA).astype(np.float32)
    out[0] = np.concatenate([cA, cA], axis=1).T
    out[1] = np.concatenate([-sA, sA], axis=1).T
    invI = (np.float32(10000.0) ** (-np.arange(0, 64, 2, dtype=np.float32) / np.float32(64))).astype(np.float32)
    angI = (pos[:, None] * invI[None, :]).astype(np.float32)
    cI, sI = np.cos(angI).astype(np.float32), np.sin(angI).astype(np.float32)
    out[2] = np.concatenate([cI, cI, cI, cI], axis=1).T
    out[3] = np.concatenate([-sI, sI, -sI, sI], axis=1).T
    return out


def _cmat():
    ident = np.eye(128, dtype=np.float32)
    ones = np.ones((128, 128), np.float32)
    ones64 = np.zeros((128, 128), np.float32)
    ones64[:64, :64] = 1
    ones64[64:, 64:] = 1
    R128 = np.zeros((128, 128), np.float32)
    R64 = np.zeros((128, 128), np.float32)
    for d in range(128):
        R128[(d + 64) % 128, d] = 1
        R64[(d // 64) * 64 + ((d % 64) + 32) % 64, d] = 1
    t = np.arange(128)
    dmask = np.where((t[None, :] // 64) <= (t[:, None] // 64), 0.0, NEG).astype(np.float32)
    return np.concatenate([ident, ones, ones64, R128, R64, dmask], axis=1)


def _prep(inputs, NT_CTX=8, NT_OWN=8):
    f = lambda k: np.asarray(inputs[k], dtype=np.float32)
    x = f("x")
    col = lambda v: np.ascontiguousarray(v.reshape(-1, 128).T)
    vecs = np.concatenate([col(f("ffn1_norm")[0]), col(f("mix_norm")[0]), col(f("ffn2_norm")[0]), col(f("gate_bias")[0]),
                           f("q_norm")[0].reshape(128, 1), f("k_norm")[0].reshape(128, 1),
                           np.tile(f("idx_k_norm")[0], 2).reshape(128, 1)], axis=1).astype(np.float32)
    shared = {
        "f1w1": f("ffn1_w1")[0], "f1w3": f("ffn1_w3")[0], "f1w2": f("ffn1_w2")[0],
        "f2w1": f("ffn2_w1")[0], "f2w3": f("ffn2_w3")[0], "f2w2": f("ffn2_w2")[0],
        "w_in": f("w_in")[0], "w_a": f("w_br_a")[0], "w_b": f("w_br_b")[0], "w_o": f("w_out")[0],
        "ws": f("gmlp_ws")[0], "bs": f("gmlp_bs")[0].reshape(1, 2048), "gvn": f("gmlp_v_norm")[0].reshape(1, 2048),
        "vecs": np.ascontiguousarray(vecs), "cmat": _cmat(),
    }
    in_maps, rows = [], []
    for c in range(8):
        b, r = c // 2, c % 2
        ctx, own = _core_blocks(r)
        tok = np.concatenate([np.arange(p * 128, (p + 1) * 128) for p in ctx + own])
        m = dict(shared)
        m["x_all"] = np.ascontiguousarray(x[b][tok])
        m["tabs"] = _rope_tabs(tok)
        cb = np.zeros((128, 16), np.float32)
        for j in range(16):
            if not (own[j] > ctx[j]):
                cb[:, j] = NEG
        m["ctxb"] = cb
        in_maps.append(m)
        rows.append((b, tok[2048:]))
    return in_maps, rows


_NC_CACHE = {}


def kernel(**inputs):
    if "nc" not in _NC_CACHE:
        _NC_CACHE["nc"] = build()
    nc = _NC_CACHE["nc"]
    in_maps, rows = _prep(inputs)
    res = run_bass_kernel_spmd(nc, in_maps, core_ids=list(range(8)))
    out = np.zeros((4, 4096, 4096), np.float32)
    for c in range(8):
        b, tok = rows[c]
        out[b, tok] = res.results[c]["out"]
    return out
```

```python
import numpy as np
from contextlib import ExitStack
import concourse.bass as bass
import concourse.mybir as mybir
from concourse.bass_utils import run_bass_kernel_spmd

F32 = mybir.dt.float32
BF16 = mybir.dt.bfloat16
ALU = mybir.AluOpType
AF = mybir.ActivationFunctionType
AX = mybir.AxisListType

D = 4096
DFF = 11008
NFF = DFF // 128
T = 256
EPS = 1e-6
C_U, C_V, C_Q, C_K, C_VV, C_QI, C_KI, C_WI, C_GA, C_GB = 0, 2048, 4096, 6144, 6272, 6400, 8448, 8512, 8544, 12640
IN_W = 16736
NEG = -1.0e30
TOPK = 256
SHAPE_OVR = {}
DEBUG = False
MIX = 9
WCACHE = True
WSC_COLS = 1000000
WSC_N = 4


class Op:
    __slots__ = ("eng", "fn", "deps", "signal", "idx", "chan", "target", "is_dma", "rk", "wk", "pos")

    def __init__(self, eng, fn):
        self.eng = eng
        self.fn = fn
        self.deps = []
        self.signal = False
        self.idx = None
        self.chan = None
        self.target = None
        self.is_dma = False


class Prog:
    ENGS = ("pe", "act", "dve", "pool", "sp")

    def __init__(self):
        self.ops = {e: [] for e in self.ENGS}
        self.last_w = {}
        self.readers = {}
        self.chan_tot = {}

    def _add(self, op, reads, writes):
        writes = list(writes) + [k for k in reads if k.startswith("ps")]
        reads = [k for k in reads if not k.startswith("ps")]
        deps = []
        for k in reads:
            w = self.last_w.get(k)
            if w is not None:
                deps.append(w)
        for k in writes:
            w = self.last_w.get(k)
            if w is not None:
                deps.append(w)
            lastr = {}
            for r in self.readers.get(k, ()):
                if r.is_dma:
                    deps.append(r)
                elif r.eng != op.eng or op.is_dma:
                    lastr[r.eng] = r
            deps.extend(lastr.values())
        seen = set()
        for d in deps:
            if id(d) in seen or d is op:
                continue
            seen.add(id(d))
            if d.eng == "pe" and op.eng == "pe":
                continue
            op.deps.append(d)
            if not d.is_dma:
                d.signal = True
        for k in reads:
            self.readers.setdefault(k, []).append(op)
        for k in writes:
            self.last_w[k] = op
            self.readers[k] = []
        op.rk, op.wk = list(reads), list(writes)
        op.pos = sum(len(v) for v in self.ops.values())
        self.ops[op.eng].append(op)
        return op

    def op(self, eng, fn, reads=(), writes=()):
        return self._add(Op(eng, fn), reads, writes)

    def dma(self, eng, chan, fn, reads=(), writes=()):
        o = Op(eng, fn)
        o.is_dma = True
        o.chan = chan
        self.chan_tot[chan] = self.chan_tot.get(chan, 0) + 16
        o.target = self.chan_tot[chan]
        return self._add(o, reads, writes)

    def barrier(self):
        lasts = {}
        for e in ("pe", "act", "dve"):
            real = [o for o in reversed(self.ops[e][-64:]) if o.fn is not None]
            if not real:
                real = [o for o in reversed(self.ops[e]) if o.fn is not None]
            if real:
                lasts[e] = real[0]
                real[0].signal = True
        chans = dict(self.chan_tot)
        for e in self.ENGS:
            o = Op(e, None)
            o.deps = [lasts[x] for x in lasts if x != e]
            o.chan = chans
            o.rk, o.wk, o.pos = [], [], -1
            self.ops[e].append(o)

    def emit(self, nc, es):
        engmap = {"pe": "tensor", "act": "scalar", "dve": "vector", "pool": "gpsimd", "sp": "sync"}
        sems = {e: es.enter_context(nc.semaphore("s_" + e)) for e in self.ENGS}
        csems = {c: es.enter_context(nc.semaphore("c_" + c)) for c in self.chan_tot}
        for e in self.ENGS:
            n = 0
            for o in self.ops[e]:
                if o.signal and not o.is_dma:
                    n += 1
                    o.idx = n
        block = es.enter_context(nc.Block())
        prog = self

        def run(ename):
            def body(eng):
                waited = {}
                for o in prog.ops[ename]:
                    for d in o.deps:
                        if d.is_dma:
                            key, val, sem = ("c", d.chan), d.target, csems[d.chan]
                        else:
                            key, val, sem = ("e", d.eng), d.idx, sems[d.eng]
                        if waited.get(key, 0) >= val:
                            continue
                        waited[key] = val
                        eng.wait_ge(sem, val)
                    if o.fn is None:
                        if isinstance(o.chan, dict):
                            for c_, tot in o.chan.items():
                                if waited.get(("c", c_), 0) < tot:
                                    waited[("c", c_)] = tot
                                    eng.wait_ge(csems[c_], tot)
                        assert not o.signal
                        continue
                    ins = o.fn(eng)
                    if o.is_dma:
                        ins.then_inc(csems[o.chan], 16)
                    elif o.signal:
                        ins.then_inc(sems[ename], 1)
            return body

        for e in self.ENGS:
            getattr(block, engmap[e])(run(e))


def build(NT_CTX=8, NT_OWN=8, STAGE=2, DBG=9):
    nc = bass.Bass("TRN2", target_bir_lowering=False)

    def di(name, shape, dt=F32):
        return nc.dram_tensor(name, SHAPE_OVR.get(name, shape), dt, kind="ExternalInput").ap()

    x_all = di("x_all", [4096, D])
    w1a, w3a, w2a = di("f1w1", [D, DFF]), di("f1w3", [D, DFF]), di("f1w2", [DFF, D])
    w1b, w3b, w2b = di("f2w1", [D, DFF]), di("f2w3", [D, DFF]), di("f2w2", [DFF, D])
    w_in = di("w_in", [D, IN_W])
    w_a, w_b, w_o = di("w_a", [2048, D]), di("w_b", [2048, D]), di("w_o", [D, D])
    ws_d = di("ws", [16, 128, 128])
    bs_d = di("bs", [1, 2048])
    gvn_d = di("gvn", [1, 2048])
    vecs_d = di("vecs", [128, 32 * 3 + 64 + 3])
    cmat_d = di("cmat", [128, 6 * 128])
    tabs_d = di("tabs", [4, 128, 4096])
    ctxb_d = di("ctxb", [128, 16])
    out_d = nc.dram_tensor("out", [2048, D], F32, kind="ExternalOutput").ap()
    h1s = nc.dram_tensor("h1s", [128, 32, 2048], F32).ap()
    nsd = nc.dram_tensor("nsd", [128, 32, 2048], BF16).ap()
    wscs = [nc.dram_tensor(f"wsc{i}", [128, WSC_COLS], BF16).ap() for i in range(WSC_N)]

    es = ExitStack()
    with es:
        sb = lambda name, shape, dt: es.enter_context(nc.sbuf_tensor(name, shape, dt))
        RA = sb("RA", [128, 8192], F32)
        RB = sb("RB", [128, 32, 256], BF16)
        RC = sb("RC", [128, 44 * 256], BF16)
        RD = [sb(f"RD{i}", [128, 8192], BF16) for i in range(3)]
        RE = sb("RE", [128, 12288], BF16)
        RF = sb("RF", [128, 12288], BF16)
        vecs = sb("vecs_sb", [128, 163], F32)
        cmat = sb("cmat_sb", [128, 768], F32)
        cbf = sb("cbf", [128, 256], BF16)
        ctxb = sb("ctxb_sb", [128, 16], F32)
        wsT = sb("wsT", [128, 16, 128], BF16)
        tabs = sb("tabs_sb", [128, 4, 256], F32)
        tmp = sb("tmpf", [128, 12, 256], F32)
        small = sb("small", [128, 64], F32)
        wabs = sb("wabs", [128, 2, 32], F32)
        wsgn = sb("wsgn", [128, 2, 32], F32)
        pst = [es.enter_context(nc.psum_tensor(f"ps{i}", [128, 512], F32)) for i in range(8)]

        hT = RA[:, :].rearrange("p (a b) -> p a b", b=256)
        vg = RA[:, 0:4096].rearrange("p (s c) -> p s c", c=2048)
        score = RA[:, 0:4096]
        yaT = RA[:, 4096:6144].bitcast(BF16).rearrange("p (a b) -> p a b", b=256)
        ybT = RA[:, 6144:8192].bitcast(BF16).rearrange("p (a b) -> p a b", b=256)
        xnT = RB
        hid = RC[:, :].rearrange("p (a b) -> p a b", b=256)
        xst = RC[:, 0:8192].bitcast(F32).rearrange("p (s c) -> p s c", c=2048)
        bs_bc = RC[:, 0:4096].bitcast(F32)
        gvn_bc = RC[:, 4096:8192].bitcast(F32)
        msk = RC[:, 0:4096]
        dg = RC[:, 4096:8192].rearrange("p (a b) -> p a b", b=128)
        rl_r = RC[:, 8192:9216].rearrange("p (a b) -> p a b", b=512)
        ex_r = RC[:, 9216:10240].rearrange("p (a b) -> p a b", b=512)
        pm_r = RC[:, 10240:11264].rearrange("p (a b) -> p a b", b=512)
        mT = RC[:, 0:8192].rearrange("p (a b) -> p a b", b=256)
        qT2 = RE[:, 0:4096].rearrange("p (s h t) -> p s h t", s=2, h=16)
        qiT = RE[:, 4096:8192].rearrange("p (a b) -> p a b", b=256)
        v_tm = RE[:, 8192:12288].rearrange("p (s c) -> p s c", c=2048)
        maskT = RE[:, 8192:12288].rearrange("p (a b) -> p a b", b=128)
        KT = RF[:, 0:4096]
        KIT = RF[:, 4096:8192]
        VV = RF[:, 8192:12288].rearrange("p (a b) -> p a b", b=128)
        ident_f, ones_f, ones64, R128, R64, dmask = [cmat[:, i * 128:(i + 1) * 128] for i in range(6)]
        ident_b, ones_b = cbf[:, 0:128], cbf[:, 128:256]
        g1, gm, g2 = vecs[:, 0:32], vecs[:, 32:64], vecs[:, 64:96]
        gateb = vecs[:, 96:160]
        qn, kn, ikn = vecs[:, 160:161], vecs[:, 161:162], vecs[:, 162:163]

        P = Prog()
        act = lambda fn, r, w: P.op("act", fn, r, w)
        dve = lambda fn, r, w: P.op("dve", fn, r, w)
        pe = lambda fn, r, w: P.op("pe", fn, r, w)

        P.dma("sp", "c0", lambda e: e.dma_start(out=vecs[:, :], in_=vecs_d[:, :]), writes=["vecs"])
        P.dma("sp", "c1", lambda e: e.dma_start(out=cmat[:, :], in_=cmat_d[:, :]), writes=["cmat"])
        P.dma("sp", "c2", lambda e: e.dma_start(out=ctxb[:, :], in_=ctxb_d[:, :]), writes=["ctxb"])
        dve(lambda e: e.tensor_copy(out=cbf[:, 0:128], in_=ident_f), ["cmat"], ["cbf"])
        dve(lambda e: e.memset(cbf[:, 128:256], 1.0), [], ["cbf1"])
        CK = ["vecs", "cmat", "cbf", "cbf1", "ctxb"]

        cnt = {"pp": 0, "pt": 0, "wc": 0, "t": 0}

        def next_pp():
            bank = 4 + cnt["pp"] % 4
            cnt["pp"] += 1
            return pst[bank][:, 0:256], f"ps{bank}"

        next_pt = next_pp

        def tmpf(i):
            return tmp[:, i, :], f"tmp{i}"

        wcache = {}
        wsc_next = [0, 0]

        def load_w(pieces, kcs, width=256):
            s_ = cnt["wc"] % 3
            cnt["wc"] += 1
            L = kcs * width
            view = RD[s_][:, 0:L].rearrange("p (a b) -> p a b", b=width)
            key = f"wc{s_}"
            ck = tuple((src.tensor.name, str(src.offset), str(src.ap), off, n) for (src, off, n) in pieces) + (kcs, width)
            if ck in wcache:
                wsc, o, bid = wcache[ck]
                P.dma("pool", f"wc{s_}", lambda e: e.dma_start(out=RD[s_][:, 0:L], in_=wsc[:, o:o + L]), reads=[f"wsc{bid}"], writes=[key])
                return view, [key]
            for i, (src, off, n) in enumerate(pieces):
                P.dma("pool", f"wc{s_}", lambda e, src=src, off=off, n=n: e.dma_start(
                    out=view[:, :, off:off + n], in_=src.rearrange("(a p) n -> p a n", p=128)), writes=[key])
            if WCACHE:
                if wsc_next[1] + L > WSC_COLS:
                    wsc_next[0] += 1
                    wsc_next[1] = 0
                wsc, o, bid = wscs[wsc_next[0]], wsc_next[1], len(wcache)
                wsc_next[1] += L
                wcache[ck] = (wsc, o, bid)
                P.dma("sp", f"wb{s_}", lambda e: e.dma_start(out=wsc[:, o:o + L], in_=RD[s_][:, 0:L]), reads=[key], writes=[f"wsc{bid}"])
            return view, [key]

        def rmsnorm_fm(gain, out_t, out_pref):
            st, stk = next_pp()
            for kc in range(32):
                sq, sqk = tmpf(kc % 2)
                act(lambda e, kc=kc, sq=sq: e.activation(out=sq, in_=hT[:, kc, :], func=AF.Square), [f"hT{kc}"], [sqk])
                pe(lambda e, kc=kc, sq=sq: e.matmul(st, lhsT=ones_f, rhs=sq, start=(kc == 0), stop=(kc == 31)),
                   [sqk, "cmat"], [stk])
            rs, rsk = tmpf(2)
            dve(lambda e: e.tensor_scalar(out=rs, in0=st, scalar1=1.0 / D, scalar2=EPS, op0=ALU.mult, op1=ALU.add), [stk], [rsk])
            act(lambda e: e.activation(out=rs, in_=rs, func=AF.Sqrt), [rsk], [rsk])
            dve(lambda e: e.reciprocal(out=rs, in_=rs), [rsk], [rsk])
            for kc in range(32):
                dve(lambda e, kc=kc: e.scalar_tensor_tensor(out=out_t[:, kc, :], in0=hT[:, kc, :], scalar=gain[:, kc:kc + 1],
                                                           in1=rs, op0=ALU.mult, op1=ALU.mult),
                    [f"hT{kc}", rsk, "vecs"], [f"{out_pref}{kc}"])

        def ffn(W1, W3, W2, gain):
            rmsnorm_fm(gain, xnT, "xn")
            first_dbg = dbg_state["n"] == 0
            dbg_state["n"] += 1
            if first_dbg:
                dump(tmp[:, 2, :], 0, ["tmp2"])
                dump(xnT[:, 0, :], 256, ["xn0"])
                dump(xnT[:, 31, :], 1280, ["xn31"])
            for (f0, f1) in ([(a_, b_) for (a_, b_) in ((0, min(44, NFF)), (min(44, NFF), NFF)) if b_ > a_] if DBG >= 2 else ()):
                f = f0
                while f < f1:
                    nf = min(2, f1 - f)
                    vA, kA = load_w([(W1[:, f * 128:(f + nf) * 128], 0, nf * 128)], 32)
                    vB, kB = load_w([(W3[:, f * 128:(f + nf) * 128], 0, nf * 128)], 32)
                    for j in range(nf):
                        psA, ka = next_pp()
                        psB, kb = next_pp()
                        for kc in range(32):
                            pe(lambda e, kc=kc, j=j, vA=vA, psA=psA: e.matmul(psA, lhsT=vA[:, kc, j * 128:(j + 1) * 128], rhs=xnT[:, kc, :],
                                                                             start=(kc == 0), stop=(kc == 31)), kA + [f"xn{kc}"], [ka])
                        for kc in range(32):
                            pe(lambda e, kc=kc, j=j, vB=vB, psB=psB: e.matmul(psB, lhsT=vB[:, kc, j * 128:(j + 1) * 128], rhs=xnT[:, kc, :],
                                                                             start=(kc == 0), stop=(kc == 31)), kB + [f"xn{kc}"], [kb])
                        sa, sak = tmpf(3 + (f + j) % 2)
                        act(lambda e, psA=psA, sa=sa: e.activation(out=sa, in_=psA, func=AF.Silu), [ka], [sak])
                        hi = f + j - f0
                        dve(lambda e, hi=hi, sa=sa, psB=psB: e.tensor_tensor(out=hid[:, hi, :], in0=sa, in1=psB, op=ALU.mult),
                            [sak, kb], [f"hid{hi}"])
                    f += nf
                if first_dbg and f0 == 0:
                    dump(hid[:, 0, :], 512, ["hid0"])
                    dump(hid[:, 1, :], 768, ["hid1"])
                for dq in range(8 if DBG >= 3 else 0):
                    ft = f0
                    while ft < f1:
                        nfc = min(16, f1 - ft)
                        vW, kW = load_w([(W2[ft * 128:(ft + nfc) * 128, dq * 512:(dq + 1) * 512], 0, 512)], nfc, width=512)
                        for fi in range(nfc):
                            fg = ft + fi
                            for d in range(4):
                                acc = pst[d][:, 0:256]
                                pe(lambda e, acc=acc, vW=vW, fi=fi, d=d, hi2=fg - f0, st_=(fg == f0), sp_=(fg == f1 - 1): e.matmul(
                                    acc, lhsT=vW[:, fi, d * 128:(d + 1) * 128], rhs=hid[:, hi2, :], start=st_, stop=sp_),
                                   kW + [f"hid{fg - f0}"], [f"ps{d}"])
                        ft += nfc
                    for d in range(4):
                        acc = pst[d][:, 0:256]
                        c = dq * 4 + d
                        dve(lambda e, acc=acc, c=c: e.scalar_tensor_tensor(out=hT[:, c, :], in0=acc, scalar=0.5, in1=hT[:, c, :],
                                                                         op0=ALU.mult, op1=ALU.add),
                            [f"ps{d}", f"hT{c}"], [f"hT{c}"])

        def normrope(ps, psk, w_, gain, ones_ap, Dn, R_ap, ti_cos, ti_sin, out_ap, out_keys):
            r = cnt["t"] % 2
            cnt["t"] += 1
            qr, qrk = tmpf(5 + r)
            act(lambda e: e.copy(out=qr[0:w_, :], in_=ps), [psk], [qrk])
            if gain is not None:
                sq, sqk = tmpf(7 + r)
                act(lambda e: e.activation(out=sq[0:w_, :], in_=ps, func=AF.Square), [psk], [sqk])
                st, stk = next_pp()
                pe(lambda e: e.matmul(st[0:w_, :], lhsT=ones_ap[0:w_, 0:w_], rhs=sq[0:w_, :], start=True, stop=True), [sqk, "cmat"], [stk])
                rs, rsk = tmpf(9 + r)
                dve(lambda e: e.tensor_scalar(out=rs[0:w_, :], in0=st[0:w_, :], scalar1=1.0 / Dn, scalar2=EPS, op0=ALU.mult, op1=ALU.add), [stk], [rsk])
                act(lambda e: e.activation(out=rs[0:w_, :], in_=rs[0:w_, :], func=AF.Sqrt), [rsk], [rsk])
                dve(lambda e: e.reciprocal(out=rs[0:w_, :], in_=rs[0:w_, :]), [rsk], [rsk])
                dve(lambda e: e.scalar_tensor_tensor(out=qr[0:w_, :], in0=qr[0:w_, :], scalar=gain[0:w_, :], in1=rs[0:w_, :],
                                                     op0=ALU.mult, op1=ALU.mult), [qrk, rsk, "vecs"], [qrk])
            sw, swk = next_pp()
            pe(lambda e: e.matmul(sw[0:w_, :], lhsT=R_ap[0:w_, 0:w_], rhs=qr[0:w_, :], start=True, stop=True), [qrk, "cmat"], [swk])
            t1, t1k = tmpf(7 + r)
            dve(lambda e: e.tensor_tensor(out=t1[0:w_, :], in0=qr[0:w_, :], in1=tabs[0:w_, ti_cos, :], op=ALU.mult), [qrk, "tabs"], [t1k])
            t2, t2k = tmpf(9 + r)
            dve(lambda e: e.tensor_tensor(out=t2[0:w_, :], in0=sw[0:w_, :], in1=tabs[0:w_, ti_sin, :], op=ALU.mult), [swk, "tabs"], [t2k])
            if out_ap.ndim == 2:
                dve(lambda e: e.tensor_tensor(out=out_ap, in0=t1[0:w_, :], in1=t2[0:w_, :], op=ALU.add), [t1k, t2k], out_keys)
            else:
                dve(lambda e: e.tensor_tensor(out=out_ap, in0=t1[0:w_, :].rearrange("p (s t) -> p s t", s=2),
                                              in1=t2[0:w_, :].rearrange("p (s t) -> p s t", s=2), op=ALU.add), [t1k, t2k], out_keys)

        def load_x_tile(r0):
            for half in range(2):
                for sub in range(2):
                    P.dma("sp", f"xs{sub}", lambda e, sub=sub, half=half: e.dma_start(
                        out=xst[:, sub, :], in_=x_all[r0 + sub * 128:r0 + (sub + 1) * 128, half * 2048:(half + 1) * 2048]),
                        writes=[f"xs{sub}"])
                for kcl in range(16):
                    kc = half * 16 + kcl
                    pt, ptk = next_pt()
                    for sub in range(2):
                        pe(lambda e, pt=pt, sub=sub, kcl=kcl: e.transpose(pt[:, sub * 128:(sub + 1) * 128], xst[:, sub, kcl * 128:(kcl + 1) * 128], ident_f),
                           [f"xs{sub}", "cmat"], [ptk])
                    if kc % 2 == 0:
                        act(lambda e, pt=pt, kc=kc: e.copy(out=hT[:, kc, :], in_=pt), [ptk], [f"hT{kc}"])
                    else:
                        dve(lambda e, pt=pt, kc=kc: e.tensor_copy(out=hT[:, kc, :], in_=pt), [ptk], [f"hT{kc}"])

        def store_out_tile(r0):
            P.barrier()
            for half in range(2):
                for kcl in range(16):
                    kc = half * 16 + kcl
                    for sub in range(2):
                        pt, ptk = next_pt()
                        pe(lambda e, pt=pt, sub=sub, kc=kc: e.transpose(pt[:, 0:128], hT[:, kc, sub * 128:(sub + 1) * 128], ident_f),
                           [f"hT{kc}", "cmat"], [ptk])
                        if sub == 0:
                            act(lambda e, pt=pt, kcl=kcl, sub=sub: e.copy(out=xst[:, sub, kcl * 128:(kcl + 1) * 128], in_=pt[:, 0:128]), [ptk], [f"xs{sub}"])
                        else:
                            dve(lambda e, pt=pt, kcl=kcl, sub=sub: e.tensor_copy(out=xst[:, sub, kcl * 128:(kcl + 1) * 128], in_=pt[:, 0:128]), [ptk], [f"xs{sub}"])
                for sub in range(2):
                    P.dma("sp", f"xs{sub}", lambda e, sub=sub, half=half: e.dma_start(
                        out=out_d[r0 + sub * 128:r0 + (sub + 1) * 128, half * 2048:(half + 1) * 2048], in_=xst[:, sub, :]),
                        reads=[f"xs{sub}"], writes=[f"outd{r0}_{half}_{sub}"])
                    OUTK.append(f"outd{r0}_{half}_{sub}")
            P.barrier()

        OUTK = []
        dbg_state = {"n": 0}
        dbg_d = nc.dram_tensor("dbg", [128, 8192], F32, kind="ExternalOutput").ap() if DEBUG else None

        def dump(ap, c0, reads):
            if not DEBUG:
                return
            n = ap.shape[-1]
            k = f"dbg{c0}"
            P.dma("pool", "dbgc", lambda e: e.dma_start(out=dbg_d[0:ap.shape[0], c0:c0 + n], in_=ap), reads=reads, writes=[k])
            OUTK.append(k)

        def load_tabs(c0):
            P.dma("sp", "tabs", lambda e: e.dma_start(out=tabs[:, :, :], in_=tabs_d[:, :, c0:c0 + 256].rearrange("f p t -> p f t")), writes=["tabs"])

        def phase1_tile(ti, own):
            r0 = ti * 256
            P.barrier()
            load_x_tile(r0)
            if DBG >= 1:
                ffn(w1a, w3a, w2a, g1)
            if STAGE == 1:
                return
            if own:
                oc = (ti - 8) * 256
                P.dma("sp", "h1o", lambda e: e.dma_start(out=h1s[:, :, oc:oc + 256], in_=hT[:, :, :]),
                      reads=[f"hT{k}" for k in range(32)], writes=[f"h1s{ti}"])
            rmsnorm_fm(gm, xnT, "xn")
            if own:
                P.dma("sp", "nso", lambda e: e.dma_start(out=nsd[:, :, oc:oc + 256], in_=xnT[:, :, :]),
                      reads=[f"xn{k}" for k in range(32)], writes=[f"ns{ti}"])
            load_tabs(r0)
            vK, kK = load_w([(w_in[:, C_K:C_K + 256], 0, 256)], 32)
            psk_, pkk = next_pp()
            for kc in range(32):
                pe(lambda e, kc=kc: e.matmul(psk_, lhsT=vK[:, kc, 0:128], rhs=xnT[:, kc, :], start=(kc == 0), stop=(kc == 31)),
                   kK + [f"xn{kc}"], [pkk])
            normrope(psk_, pkk, 128, kn, ones_f, 128.0, R128, 0, 1, KT[:, r0:r0 + 256], [f"KT{ti}"])
            psv, pvk = next_pp()
            for sub in range(2):
                for kc in range(32):
                    pe(lambda e, kc=kc, sub=sub: e.matmul(psv[:, sub * 128:(sub + 1) * 128], lhsT=xnT[:, kc, sub * 128:(sub + 1) * 128],
                                                          rhs=vK[:, kc, 128:256], start=(kc == 0), stop=(kc == 31)),
                       kK + [f"xn{kc}"], [pvk])
            act(lambda e: e.copy(out=VV[:, 2 * ti:2 * ti + 2, :], in_=psv.rearrange("p (s c) -> p s c", s=2)), [pvk], [f"VV{ti}"])
            vI, kI = load_w([(w_in[:, C_KI:C_KI + 64], 0, 64), (w_in[:, C_KI:C_KI + 64], 64, 64)], 32)
            psi, pik = next_pp()
            for kc in range(32):
                pe(lambda e, kc=kc: e.matmul(psi, lhsT=vI[:, kc, 0:128], rhs=xnT[:, kc, :], start=(kc == 0), stop=(kc == 31)),
                   kI + [f"xn{kc}"], [pik])
            normrope(psi, pik, 128, ikn, ones64, 64.0, R64, 2, 3, KIT[:, r0:r0 + 256], [f"KIT{ti}"])

        def mixer_tile(i):
            ti = 8 + i
            oc = i * 256
            r0 = ti * 256
            P.barrier()
            P.dma("sp", "nsi", lambda e: e.dma_start(out=xnT[:, :, :], in_=nsd[:, :, oc:oc + 256]),
                  reads=[f"ns{ti}"], writes=[f"xn{k}" for k in range(32)])
            load_tabs(r0)
            P.dma("sp", "bsb", lambda e: e.dma_start(out=bs_bc, in_=bs_d[0:1, :].to_broadcast([128, 2048])), writes=["bs_bc"])
            P.dma("sp", "gvb", lambda e: e.dma_start(out=gvn_bc, in_=gvn_d[0:1, :].to_broadcast([128, 2048])), writes=["gvn_bc"])
            XN = [f"xn{k}" for k in range(32)]

            def bail():
                P.barrier()
                P.dma("sp", "h1i", lambda e: e.dma_start(out=hT[:, :, :], in_=h1s[:, :, oc:oc + 256]), reads=[f"h1s{ti}"], writes=[f"hT{k}" for k in range(32)])
                P.barrier()

            if MIX < 1:
                return bail()
            for cb in range(8):
                vW, kW = load_w([(w_in[:, C_V + cb * 256:C_V + (cb + 1) * 256], 0, 256)], 32)
                for sub in range(2):
                    ps, pk = next_pp()
                    for kc in range(32):
                        pe(lambda e, kc=kc, sub=sub, ps=ps, vW=vW: e.matmul(ps, lhsT=xnT[:, kc, sub * 128:(sub + 1) * 128], rhs=vW[:, kc, :],
                                                                           start=(kc == 0), stop=(kc == 31)), kW + [f"xn{kc}"], [pk])
                    act(lambda e, ps=ps, sub=sub, cb=cb: e.activation(out=vg[:, sub, cb * 256:(cb + 1) * 256], in_=ps, func=AF.Gelu), [pk], [f"vg{sub}_{cb}"])
            for sub in range(2):
                VG = [f"vg{sub}_{cb}" for cb in range(8)]
                dve(lambda e, sub=sub: e.memset(small[:, sub:sub + 1], 0.0), [], [f"ssv{sub}"])
                act(lambda e, sub=sub: e.activation(out=RA[:, 4096:6144], in_=vg[:, sub, :], func=AF.Square,
                                                   accum_out=small[:, sub:sub + 1]), VG + [f"ssv{sub}"], [f"ssv{sub}"] + [f"ya{g}" for g in range(16)])
                dve(lambda e, sub=sub: e.tensor_scalar(out=small[:, 2 + sub:3 + sub], in0=small[:, sub:sub + 1], scalar1=1.0 / 2048, scalar2=EPS,
                                                      op0=ALU.mult, op1=ALU.add), [f"ssv{sub}"], [f"rsv{sub}"])
                act(lambda e, sub=sub: e.activation(out=small[:, 2 + sub:3 + sub], in_=small[:, 2 + sub:3 + sub], func=AF.Sqrt), [f"rsv{sub}"], [f"rsv{sub}"])
                dve(lambda e, sub=sub: e.reciprocal(out=small[:, 2 + sub:3 + sub], in_=small[:, 2 + sub:3 + sub]), [f"rsv{sub}"], [f"rsv{sub}"])
                dve(lambda e, sub=sub: e.scalar_tensor_tensor(out=v_tm[:, sub, :], in0=vg[:, sub, :], scalar=small[:, 2 + sub:3 + sub], in1=gvn_bc,
                                                             op0=ALU.mult, op1=ALU.mult), VG + [f"rsv{sub}", "gvn_bc"], [f"vtm{sub}"])
            for cb in range(8):
                vW, kW = load_w([(w_in[:, C_U + cb * 256:C_U + (cb + 1) * 256], 0, 256)], 32)
                for j in range(2):
                    g = cb * 2 + j
                    ps, pk = next_pp()
                    for kc in range(32):
                        pe(lambda e, kc=kc, j=j, ps=ps, vW=vW: e.matmul(ps, lhsT=vW[:, kc, j * 128:(j + 1) * 128], rhs=xnT[:, kc, :],
                                                                       start=(kc == 0), stop=(kc == 31)), kW + [f"xn{kc}"], [pk])
                    gu, guk = tmpf(3 + g % 2)
                    act(lambda e, ps=ps, gu=gu: e.activation(out=gu, in_=ps, func=AF.Gelu), [pk], [guk])
                    pm_, pmk = next_pt()
                    for sub in range(2):
                        pe(lambda e, pm_=pm_, sub=sub, g=g: e.matmul(pm_[:, sub * 128:(sub + 1) * 128], lhsT=v_tm[:, sub, g * 128:(g + 1) * 128],
                                                                    rhs=wsT[:, g, :], start=True, stop=True), [f"vtm{sub}", "wsT"], [pmk])
                    tt, ttk = tmpf(5 + g % 2)
                    dve(lambda e, pm_=pm_, tt=tt, g=g: e.tensor_tensor(out=tt.rearrange("p (s t) -> p s t", s=2), in0=pm_.rearrange("p (s t) -> p s t", s=2),
                                                                      in1=bs_bc[:, g * 128:(g + 1) * 128].unsqueeze(1).to_broadcast([128, 2, 128]), op=ALU.add),
                        [pmk, "bs_bc"], [ttk])
                    dve(lambda e, tt=tt, gu=gu, g=g: e.tensor_tensor(out=yaT[:, g, :], in0=tt, in1=gu, op=ALU.mult), [ttk, guk], [f"ya{g}"])
            if MIX < 2:
                return bail()
            for cb in range(8):
                vW, kW = load_w([(w_in[:, C_Q + cb * 256:C_Q + (cb + 1) * 256], 0, 256)], 32)
                for j in range(2):
                    h = cb * 2 + j
                    ps, pk = next_pp()
                    for kc in range(32):
                        pe(lambda e, kc=kc, j=j, ps=ps, vW=vW: e.matmul(ps, lhsT=vW[:, kc, j * 128:(j + 1) * 128], rhs=xnT[:, kc, :],
                                                                       start=(kc == 0), stop=(kc == 31)), kW + [f"xn{kc}"], [pk])
                    normrope(ps, pk, 128, qn, ones_f, 128.0, R128, 0, 1, qT2[:, :, h, :], [f"qT{h}"])
            for cb in range(8):
                vW, kW = load_w([(w_in[:, C_QI + cb * 256:C_QI + (cb + 1) * 256], 0, 256)], 32)
                for j in range(2):
                    c = cb * 2 + j
                    ps, pk = next_pp()
                    for kc in range(32):
                        pe(lambda e, kc=kc, j=j, ps=ps, vW=vW: e.matmul(ps, lhsT=vW[:, kc, j * 128:(j + 1) * 128], rhs=xnT[:, kc, :],
                                                                       start=(kc == 0), stop=(kc == 31)), kW + [f"xn{kc}"], [pk])
                    normrope(ps, pk, 128, None, None, 0, R64, 2, 3, qiT[:, c, :], [f"qi{c}"])
            vW, kW = load_w([(w_in[:, C_WI:C_WI + 32], 0, 32)], 32)
            psw, pwk = next_pp()
            for sub in range(2):
                for kc in range(32):
                    pe(lambda e, kc=kc, sub=sub, vW=vW: e.matmul(psw[:, sub * 32:(sub + 1) * 32], lhsT=xnT[:, kc, sub * 128:(sub + 1) * 128], rhs=vW[:, kc, 0:32],
                                                          start=(kc == 0), stop=(kc == 31)), kW + [f"xn{kc}"], [pwk])
            idx_scale = (32 ** -0.5) * (64 ** -0.5)
            dve(lambda e: e.tensor_scalar(out=wsgn[:, :, :], in0=psw[:, 0:64].rearrange("p (s h) -> p s h", s=2), scalar1=0.0, scalar2=2.0,
                                          op0=ALU.is_ge, op1=ALU.mult), [pwk], ["wsgn"])
            dve(lambda e: e.tensor_scalar(out=wsgn[:, :, :], in0=wsgn[:, :, :], scalar1=-1.0, scalar2=None, op0=ALU.add), ["wsgn"], ["wsgn"])
            dve(lambda e: e.scalar_tensor_tensor(out=wabs[:, :, :], in0=psw[:, 0:64].rearrange("p (s h) -> p s h", s=2), scalar=idx_scale,
                                                 in1=wsgn[:, :, :], op0=ALU.mult, op1=ALU.mult), [pwk, "wsgn"], ["wabs"])
            P.barrier()
            if MIX < 2.1:
                return bail()
            def qblock(sub):
                j = 2 * i + sub
                nb = j + 1
                nk = 2 * nb * 128
                for h in range(32):
                    dve(lambda e, h=h, sub=sub: e.tensor_scalar(out=dg[:, h, :], in0=ident_b, scalar1=wsgn[:, sub, h:h + 1], scalar2=None, op0=ALU.mult),
                        ["wsgn", "cbf"], [f"dg{h}"])
                parts = []
                for part in range(2):
                    for b0 in range(0, nb, 4):
                        nbb = min(4, nb - b0)
                        parts.append((part * 2048 + b0 * 128, part * nb * 128 + b0 * 128, nbb * 128))
                rl4 = RC[:, 8192:10240].rearrange("p (a b) -> p a b", b=512)
                IB = (4, 5, 0, 1)
                for pi_, (kcol, scol, ncol) in enumerate(parts):
                    pS, pSk = pst[6][:, 0:ncol], ["ps6"]

                    def emit_I(h, kcol=kcol, ncol=ncol):
                        c, hf = h // 2, h % 2
                        bank = IB[h % 4]
                        pI = pst[bank][:, 0:ncol]
                        pIk = [f"ps{bank}"]
                        pe(lambda e, pI=pI, c=c, hf=hf: e.matmul(
                            pI, lhsT=qiT[hf * 64:(hf + 1) * 64, c, sub * 128:(sub + 1) * 128], rhs=KIT[hf * 64:(hf + 1) * 64, kcol:kcol + ncol],
                            start=True, stop=True), [f"qi{c}"], pIk)
                        rl = rl4[:, h % 4, 0:ncol]
                        act(lambda e, pI=pI, rl=rl, h=h: e.activation(out=rl, in_=pI, func=AF.Relu, scale=wabs[:, sub, h:h + 1]),
                            pIk + ["wabs"], [f"rl{h % 4}"])

                    def emit_S(h, ncol=ncol, pS=pS, pSk=pSk):
                        rl = rl4[:, h % 4, 0:ncol]
                        pe(lambda e, rl=rl, h=h: e.matmul(pS, lhsT=dg[:, h, :], rhs=rl, start=(h == 0), stop=(h == 31)),
                           [f"rl{h % 4}", f"dg{h}"], pSk)

                    for h in range(3):
                        emit_I(h)
                    for h in range(32):
                        emit_S(h)
                        if h + 3 < 32:
                            emit_I(h + 3)
                    act(lambda e, pS=pS, scol=scol, ncol=ncol: e.copy(out=score[:, scol:scol + ncol], in_=pS), pSk, [f"sc{pi_}"])
                if MIX < 2.3:
                    return
                SC = [f"sc{p_}" for p_ in range(len(parts))]
                sm = lambda a: small[:, a:a + 1]
                dve(lambda e: e.tensor_reduce(out=sm(8), in_=score[:, 0:nk], axis=AX.X, op=ALU.max), SC, ["s8"])
                dve(lambda e: e.tensor_reduce(out=sm(9), in_=score[:, 0:nk], axis=AX.X, op=ALU.min), SC, ["s9"])
                dve(lambda e: e.scalar_tensor_tensor(out=sm(10), in0=sm(9), scalar=-1.0, in1=sm(8), op0=ALU.mult, op1=ALU.max), ["s8", "s9"], ["s10"])
                dve(lambda e: e.tensor_scalar(out=sm(11), in0=sm(10), scalar1=-1.001, scalar2=-1e-6, op0=ALU.mult, op1=ALU.add), ["s10"], ["lo"])
                dve(lambda e: e.tensor_scalar(out=sm(12), in0=sm(11), scalar1=-2.0, scalar2=None, op0=ALU.mult), ["lo"], ["w0"])
                cl = (nb - 1) * 128
                dve(lambda e: e.tensor_scalar(out=score[:, cl:cl + 128], in0=score[:, cl:cl + 128], scalar1=ctxb[:, j:j + 1], scalar2=None, op0=ALU.add),
                    SC + ["ctxb"], ["scb"])
                ol = nb * 128 + (nb - 1) * 128
                dve(lambda e: e.tensor_tensor(out=score[:, ol:ol + 128], in0=score[:, ol:ol + 128], in1=dmask, op=ALU.add), SC + ["cmat"], ["scc"])
                SCALL = SC + ["scb", "scc"]
                for it in range(20):
                    fac = 2.0 ** (-(it + 1))
                    dve(lambda e, fac=fac: e.scalar_tensor_tensor(out=sm(13), in0=sm(12), scalar=fac, in1=sm(11), op0=ALU.mult, op1=ALU.add), ["w0", "lo"], ["mid"])
                    dve(lambda e: e.memset(sm(14), 0.0), [], ["cnt"])
                    dve(lambda e: e.tensor_scalar(out=msk[:, 0:nk], in0=score[:, 0:nk], scalar1=sm(13), scalar2=0.0, op0=ALU.is_ge, op1=ALU.add,
                                                  accum_out=sm(14)), SCALL + ["mid", "cnt"], ["cnt", "msk"])
                    dve(lambda e: e.tensor_scalar(out=sm(15), in0=sm(14), scalar1=TOPK - 0.5, scalar2=None, op0=ALU.is_ge), ["cnt"], ["sel"])
                    dve(lambda e, fac=fac: e.scalar_tensor_tensor(out=sm(16), in0=sm(15), scalar=fac, in1=sm(12), op0=ALU.mult, op1=ALU.mult), ["sel", "w0"], ["stp"])
                    dve(lambda e: e.tensor_tensor(out=sm(11), in0=sm(11), in1=sm(16), op=ALU.add), ["lo", "stp"], ["lo"])
                dve(lambda e: e.tensor_scalar(out=msk[:, 0:nk], in0=score[:, 0:nk], scalar1=sm(11), scalar2=None, op0=ALU.is_ge), SCALL + ["lo"], ["msk"])
                if MIX < 2.5:
                    return
                ptb = pst[7][:, 0:256].bitcast(BF16)
                for kb in range(2 * nb):
                    q4 = kb % 4
                    pe(lambda e, kb=kb, q4=q4: e.transpose(ptb[:, q4 * 128:(q4 + 1) * 128], msk[:, kb * 128:(kb + 1) * 128], ident_b), ["msk", "cbf"], ["ps7"])
                    act(lambda e, kb=kb, q4=q4: e.copy(out=maskT[:, kb, :], in_=ptb[:, q4 * 128:(q4 + 1) * 128]), ["ps7"], [f"mT{kb}"])
                if MIX < 2.7:
                    return
                for hg in range(4):
                    pO, pOk = pst[0][:, :], ["ps0"]
                    pD, pDk = pst[1][:, :], ["ps1"]
                    def emit_QK(kb, hg=hg):
                        part, b = (0, kb) if kb < nb else (1, kb - nb)
                        kcol = part * 2048 + b * 128
                        pL = pst[4 + kb % 2][:, :]
                        pLk = [f"ps{4 + kb % 2}"]
                        pe(lambda e, pL=pL, kcol=kcol: e.matmul(pL, lhsT=KT[:, kcol:kcol + 128],
                                                                rhs=RE[:, sub * 2048 + hg * 512:sub * 2048 + (hg + 1) * 512], start=True, stop=True),
                           [f"qT{hh}" for hh in range(hg * 4, hg * 4 + 4)], pLk)
                        ex = ex_r[:, kb % 2, :]
                        act(lambda e, pL=pL, ex=ex: e.activation(out=ex, in_=pL, func=AF.Exp, scale=128.0 ** -0.5), pLk, [f"ex{kb % 2}"])
                        pm = pm_r[:, kb % 2, :]
                        dve(lambda e, pm=pm, ex=ex, kb=kb: e.tensor_tensor(out=pm.rearrange("p (h t) -> p h t", h=4), in0=ex.rearrange("p (h t) -> p h t", h=4),
                                                                          in1=maskT[:, kb, :].unsqueeze(1).to_broadcast([128, 4, 128]), op=ALU.mult),
                            [f"ex{kb % 2}", f"mT{kb}"], [f"pm{kb % 2}"])

                    def emit_PV(kb):
                        part, b = (0, kb) if kb < nb else (1, kb - nb)
                        vblk = part * 16 + b
                        pm = pm_r[:, kb % 2, :]
                        pe(lambda e, pm=pm, vblk=vblk, kb=kb: e.matmul(pO, lhsT=VV[:, vblk, :], rhs=pm, start=(kb == 0), stop=(kb == 2 * nb - 1)),
                           [f"pm{kb % 2}"], pOk)
                        pe(lambda e, pm=pm, kb=kb: e.matmul(pD, lhsT=ones_b, rhs=pm, start=(kb == 0), stop=(kb == 2 * nb - 1)),
                           [f"pm{kb % 2}", "cbf1"], pDk)

                    emit_QK(0)
                    for kb in range(2 * nb):
                        if kb + 1 < 2 * nb:
                            emit_QK(kb + 1)
                        emit_PV(kb)
                    rc0, rc0k = tmpf(7)
                    rc1, rc1k = tmpf(8)
                    rec = tmp[:, 7:9, :].rearrange("p a b -> p (a b)")
                    dve(lambda e: e.reciprocal(out=rec, in_=pD), pDk, [rc0k, rc1k])
                    dve(lambda e, hg=hg, sub=sub: e.tensor_tensor(out=ybT[:, hg * 4:(hg + 1) * 4, sub * 128:(sub + 1) * 128],
                                                                  in0=pO.rearrange("p (h t) -> p h t", h=4), in1=rec.rearrange("p (h t) -> p h t", h=4), op=ALU.mult),
                        pOk + [rc0k, rc1k], [f"yb{hg}_{sub}"])
            for sub_ in range(2):
                qblock(sub_)
            P.barrier()
            if MIX < 4:
                return bail()
            for cb in range(16):
                vGa, kGa = load_w([(w_in[:, C_GA + cb * 256:C_GA + (cb + 1) * 256], 0, 256)], 32)
                gts = []
                for j in range(2):
                    dch = cb * 2 + j
                    ps, pk = next_pp()
                    for kc in range(32):
                        pe(lambda e, kc=kc, j=j, ps=ps, vGa=vGa: e.matmul(ps, lhsT=vGa[:, kc, j * 128:(j + 1) * 128], rhs=xnT[:, kc, :],
                                                                         start=(kc == 0), stop=(kc == 31)), kGa + [f"xn{kc}"], [pk])
                    ga, gak = tmpf(3 + j)
                    act(lambda e, ps=ps, ga=ga, dch=dch: e.activation(out=ga, in_=ps, func=AF.Sigmoid, bias=gateb[:, dch:dch + 1]), [pk, "vecs"], [gak])
                vA, kA = load_w([(w_a[:, cb * 256:(cb + 1) * 256], 0, 256)], 16)
                for j in range(2):
                    ga, gak = tmpf(3 + j)
                    ps2, pk2 = next_pp()
                    for g in range(16):
                        pe(lambda e, g=g, j=j, ps2=ps2, vA=vA: e.matmul(ps2, lhsT=vA[:, g, j * 128:(j + 1) * 128], rhs=yaT[:, g, :],
                                                                       start=(g == 0), stop=(g == 15)), kA + [f"ya{g}"], [pk2])
                    dve(lambda e, ps2=ps2, ga=ga: e.tensor_tensor(out=ga, in0=ps2, in1=ga, op=ALU.mult), [pk2, gak], [gak])
                vGb, kGb = load_w([(w_in[:, C_GB + cb * 256:C_GB + (cb + 1) * 256], 0, 256)], 32)
                for j in range(2):
                    dch = cb * 2 + j
                    ps3, pk3 = next_pp()
                    for kc in range(32):
                        pe(lambda e, kc=kc, j=j, ps3=ps3, vGb=vGb: e.matmul(ps3, lhsT=vGb[:, kc, j * 128:(j + 1) * 128], rhs=xnT[:, kc, :],
                                                                           start=(kc == 0), stop=(kc == 31)), kGb + [f"xn{kc}"], [pk3])
                    gb, gbk = tmpf(5 + j)
                    act(lambda e, ps3=ps3, gb=gb, dch=dch: e.activation(out=gb, in_=ps3, func=AF.Sigmoid, bias=gateb[:, 32 + dch:33 + dch]), [pk3, "vecs"], [gbk])
                vB, kB = load_w([(w_b[:, cb * 256:(cb + 1) * 256], 0, 256)], 16)
                for j in range(2):
                    dch = cb * 2 + j
                    ga, gak = tmpf(3 + j)
                    gb, gbk = tmpf(5 + j)
                    ps4, pk4 = next_pp()
                    for hh in range(16):
                        pe(lambda e, hh=hh, j=j, ps4=ps4, vB=vB: e.matmul(ps4, lhsT=vB[:, hh, j * 128:(j + 1) * 128], rhs=ybT[:, hh, :],
                                                                         start=(hh == 0), stop=(hh == 15)),
                           kB + [f"yb{hh // 4}_0", f"yb{hh // 4}_1"], [pk4])
                    dve(lambda e, ps4=ps4, gb=gb: e.tensor_tensor(out=gb, in0=ps4, in1=gb, op=ALU.mult), [pk4, gbk], [gbk])
                    dve(lambda e, dch=dch, ga=ga, gb=gb: e.tensor_tensor(out=mT[:, dch, :], in0=ga, in1=gb, op=ALU.add), [gak, gbk], [f"m{dch}"])
            P.barrier()
            if MIX < 5:
                return bail()
            P.dma("sp", "h1i", lambda e: e.dma_start(out=hT[:, :, :], in_=h1s[:, :, oc:oc + 256]), reads=[f"h1s{ti}"], writes=[f"hT{k}" for k in range(32)])
            for cb in range(16):
                vO, kO = load_w([(w_o[:, cb * 256:(cb + 1) * 256], 0, 256)], 32)
                for j in range(2):
                    dch = cb * 2 + j
                    ps, pk = next_pp()
                    for kc in range(32):
                        pe(lambda e, kc=kc, j=j, ps=ps, vO=vO: e.matmul(ps, lhsT=vO[:, kc, j * 128:(j + 1) * 128], rhs=mT[:, kc, :],
                                                                       start=(kc == 0), stop=(kc == 31)), kO + [f"m{kc}"], [pk])
                    dve(lambda e, ps=ps, dch=dch: e.tensor_tensor(out=hT[:, dch, :], in0=hT[:, dch, :], in1=ps, op=ALU.add), [pk, f"hT{dch}"], [f"hT{dch}"])
            P.barrier()


        if STAGE == 1:
            for i in range(NT_OWN):
                phase1_tile(8 + i, True)
                store_out_tile(i * 256)
        else:
            for g in range(16):
                wst, wstk = tmpf(g % 2)
                P.dma("sp", f"wsl{g % 2}", lambda e, g=g, wst=wst: e.dma_start(out=wst[:, 0:128], in_=ws_d[g, :, :]), writes=[wstk])
                pt, ptk = next_pt()
                pe(lambda e, pt=pt, wst=wst: e.transpose(pt[:, 0:128], wst[:, 0:128], ident_f), [wstk, "cmat"], [ptk])
                dve(lambda e, pt=pt, g=g: e.tensor_copy(out=wsT[:, g, :], in_=pt[:, 0:128]), [ptk], ["wsT"])
                dve(lambda e, g=g: e.memset(wsT[64:128, g, 0:64], 0.0), ["wsT"], ["wsT"])
            for ti in range(NT_CTX):
                phase1_tile(ti, False)
            for i in range(NT_OWN):
                phase1_tile(8 + i, True)
            P.barrier()
            for i in range(NT_OWN):
                mixer_tile(i)
                ffn(w1b, w3b, w2b, g2)
                store_out_tile(i * 256)
        P.op("sp", None, reads=OUTK, writes=[])
        P.emit(nc, es)
        global _LASTP
        _LASTP = P
    return nc


def _owner(p):
    return (0, 1, 1, 0)[p % 4]


def _core_blocks(r):
    own = [p for p in range(32) if _owner(p) == r]
    ctx = [p for p in range(32) if _owner(p) != r]
    return ctx, own


def _rope_tabs(pos):
    pos = pos.astype(np.float32)
    out = np.zeros((4, 128, pos.shape[0]), np.float32)
    invA = (np.float32(10000.0) ** (-np.arange(0, 128, 2, dtype=np.float32) / np.float32(128))).astype(np.float32)
    angA = (pos[:, None] * invA[None, :]).astype(np.float32)
    cA, sA = np.cos(angA).astype(np.float32), np.sin(angA).astype(np.float32)
    out[0] = np.concatenate([cA, cA], axis=1).T
    out[1] = np.concatenate([-sA, sA], axis=1).T
    invI = (np.float32(10000.0) ** (-np.arange(0, 64, 2, dtype=np.float32) / np.float32(64))).astype(np.float32)
    angI = (pos[:, None] * invI[None, :]).astype(np.float32)
    cI, sI = np.cos(angI).astype(np.float32), np.sin(angI).astype(np.float32)
    out[2] = np.concatenate([cI, cI, cI, cI], axis=1).T
    out[3] = np.concatenate([-sI, sI, -sI, sI], axis=1).T
    return out


def _cmat():
    ident = np.eye(128, dtype=np.float32)
    ones = np.ones((128, 128), np.float32)
    ones64 = np.zeros((128, 128), np.float32)
    ones64[:64, :64] = 1
    ones64[64:, 64:] = 1
    R128 = np.zeros((128, 128), np.float32)
    R64 = np.zeros((128, 128), np.float32)
    for d in range(128):
        R128[(d + 64) % 128, d] = 1
        R64[(d // 64) * 64 + ((d % 64) + 32) % 64, d] = 1
    t = np.arange(128)
    dmask = np.where((t[None, :] // 64) <= (t[:, None] // 64), 0.0, NEG).astype(np.float32)
    return np.concatenate([ident, ones, ones64, R128, R64, dmask], axis=1)


def _prep(inputs, NT_CTX=8, NT_OWN=8):
    f = lambda k: np.asarray(inputs[k], dtype=np.float32)
    x = f("x")
    col = lambda v: np.ascontiguousarray(v.reshape(-1, 128).T)
    vecs = np.concatenate([col(f("ffn1_norm")[0]), col(f("mix_norm")[0]), col(f("ffn2_norm")[0]), col(f("gate_bias")[0]),
                           f("q_norm")[0].reshape(128, 1), f("k_norm")[0].reshape(128, 1),
                           np.tile(f("idx_k_norm")[0], 2).reshape(128, 1)], axis=1).astype(np.float32)
    shared = {
        "f1w1": f("ffn1_w1")[0], "f1w3": f("ffn1_w3")[0], "f1w2": f("ffn1_w2")[0],
        "f2w1": f("ffn2_w1")[0], "f2w3": f("ffn2_w3")[0], "f2w2": f("ffn2_w2")[0],
        "w_in": f("w_in")[0], "w_a": f("w_br_a")[0], "w_b": f("w_br_b")[0], "w_o": f("w_out")[0],
        "ws": f("gmlp_ws")[0], "bs": f("gmlp_bs")[0].reshape(1, 2048), "gvn": f("gmlp_v_norm")[0].reshape(1, 2048),
        "vecs": np.ascontiguousarray(vecs), "cmat": _cmat(),
    }
    in_maps, rows = [], []
    for c in range(8):
        b, r = c // 2, c % 2
        ctx, own = _core_blocks(r)
        tok = np.concatenate([np.arange(p * 128, (p + 1) * 128) for p in ctx + own])
        m = dict(shared)
        m["x_all"] = np.ascontiguousarray(x[b][tok])
        m["tabs"] = _rope_tabs(tok)
        cb = np.zeros((128, 16), np.float32)
        for j in range(16):
            if not (own[j] > ctx[j]):
                cb[:, j] = NEG
        m["ctxb"] = cb
        in_maps.append(m)
        rows.append((b, tok[2048:]))
    return in_maps, rows


_NC_CACHE = {}


def kernel(**inputs):
    if "nc" not in _NC_CACHE:
        _NC_CACHE["nc"] = build()
    nc = _NC_CACHE["nc"]
    in_maps, rows = _prep(inputs)
    res = run_bass_kernel_spmd(nc, in_maps, core_ids=list(range(8)))
    out = np.zeros((4, 4096, 4096), np.float32)
    for c in range(8):
        b, tok = rows[c]
        out[b, tok] = res.results[c]["out"]
    return out
```
